# Optimizing a Trainium2 kernel written in Bass

```python
import math
import jax, jax.numpy as jnp
from jax import lax
import numpy as np

D_MODEL = 1024
BATCH = 4
SEQ = 4096
DEPTH = 1

CHUNK = 64
MIX_WIDTH = D_MODEL
HG_WIDTH = MIX_WIDTH // 2
HG_HEAD_DIM = 128
HG_HEADS = HG_WIDTH // HG_HEAD_DIM
S5_WIDTH = MIX_WIDTH - HG_WIDTH
S5_GROUP = 16
S5_GROUPS = S5_WIDTH // S5_GROUP
S5_STATE = 64
IN_WIDTH = 4 * HG_WIDTH + S5_WIDTH
D_FF = ((8 * D_MODEL // 3) + 127) // 128 * 128
CONV_WIDTH = 3
EPS = 1e-6
DT_MIN = 1e-3
DT_MAX = 1e-1

kernel_name = "hgrn2_s5_parallel_hybrid_block"


def rmsnorm(x, g):
    xf = x.astype(jnp.float32)
    y = xf * lax.rsqrt(jnp.mean(xf * xf, axis=-1, keepdims=True) + EPS)
    return (y * g.astype(jnp.float32)).astype(x.dtype)


def hgrn2_mix(q, fz, v, gz, lb, norm_g):
    bsz, seq_len, _ = q.shape
    n_chunks = seq_len // CHUNK

    def heads(t):
        return t.astype(jnp.float32).reshape(bsz, n_chunks, CHUNK, HG_HEADS, HG_HEAD_DIM).transpose(0, 3, 1, 2, 4)

    f = lb + (1.0 - lb) * jax.nn.sigmoid(fz.astype(jnp.float32))
    logf = heads(jnp.log(f))
    k = heads(1.0 - f)
    qh = heads(q)
    vh = heads(v)

    b = jnp.cumsum(logf, axis=3)
    b_last = b[:, :, :, -1:, :]
    q_dec = qh * jnp.exp(b)
    k_dec = k * jnp.exp(-b)

    mask = jnp.tril(jnp.ones((CHUNK, CHUNK), dtype=bool))
    att = jnp.einsum('bhnck,bhnsk->bhncs', q_dec, k_dec)
    att = jnp.where(mask, att, 0.0)
    o_intra = jnp.einsum('bhncs,bhnsv->bhncv', att, vh)

    k_tail = k * jnp.exp(b_last - b)
    d_state = jnp.einsum('bhnck,bhncv->nbhkv', k_tail, vh)
    decay = jnp.exp(b_last[:, :, :, 0, :]).transpose(2, 0, 1, 3)

    def step(state, inp):
        dec, ds = inp
        return dec[..., None] * state + ds, state

    s0 = jnp.zeros((bsz, HG_HEADS, HG_HEAD_DIM, HG_HEAD_DIM), jnp.float32)
    _, s_start = lax.scan(step, s0, (decay, d_state))
    o_inter = jnp.einsum('bhnck,nbhkv->bhncv', q_dec, s_start)

    o = (o_intra + o_inter).transpose(0, 2, 3, 1, 4).reshape(bsz, seq_len, HG_HEADS, HG_HEAD_DIM)
    o = o * lax.rsqrt(jnp.mean(o * o, axis=-1, keepdims=True) + EPS)
    o = o.reshape(bsz, seq_len, HG_WIDTH) * norm_g.astype(jnp.float32) * jax.nn.silu(gz.astype(jnp.float32))
    return o.astype(q.dtype)


def s5_mix(u, a_re, a_im, log_dt, b_re, b_im, c_re, c_im, d_skip, w_glu, b_glu):
    bsz, seq_len, _ = u.shape
    uf = u.astype(jnp.float32).reshape(bsz, seq_len, S5_GROUPS, S5_GROUP)
    dt = jnp.exp(log_dt.astype(jnp.float32))[:, None]
    ar = a_re.astype(jnp.float32)
    ai = a_im.astype(jnp.float32)
    mag = jnp.exp(dt * ar)
    abar_re = mag * jnp.cos(dt * ai)
    abar_im = mag * jnp.sin(dt * ai)
    num_re = abar_re - 1.0
    num_im = abar_im
    den = ar * ar + ai * ai
    z_re = (num_re * ar + num_im * ai) / den
    z_im = (num_im * ar - num_re * ai) / den
    br = b_re.astype(jnp.float32)
    bi = b_im.astype(jnp.float32)
    bbar_re = z_re[..., None] * br - z_im[..., None] * bi
    bbar_im = z_re[..., None] * bi + z_im[..., None] * br

    bu_re = jnp.einsum('blgp,gnp->blgn', uf, bbar_re)
    bu_im = jnp.einsum('blgp,gnp->blgn', uf, bbar_im)
    a_re_t = jnp.broadcast_to(abar_re, bu_re.shape)
    a_im_t = jnp.broadcast_to(abar_im, bu_im.shape)

    def combine(e1, e2):
        a1r, a1i, x1r, x1i = e1
        a2r, a2i, x2r, x2i = e2
        return (a2r * a1r - a2i * a1i,
                a2r * a1i + a2i * a1r,
                a2r * x1r - a2i * x1i + x2r,
                a2r * x1i + a2i * x1r + x2i)

    _, _, x_re, x_im = lax.associative_scan(combine, (a_re_t, a_im_t, bu_re, bu_im), axis=1)
    y = (jnp.einsum('gpn,blgn->blgp', c_re.astype(jnp.float32), x_re)
         - jnp.einsum('gpn,blgn->blgp', c_im.astype(jnp.float32), x_im))
    y = y + d_skip.astype(jnp.float32).reshape(S5_GROUPS, S5_GROUP) * uf
    y = jax.nn.gelu(y.reshape(bsz, seq_len, S5_WIDTH))
    y = y * jax.nn.sigmoid(y @ w_glu.astype(jnp.float32) + b_glu.astype(jnp.float32))
    return y.astype(u.dtype)


def conv_ffn(x, w_up, conv_w, conv_b, w_down):
    seq_len = x.shape[1]
    hid = x @ w_up
    pad = jnp.pad(hid, ((0, 0), (CONV_WIDTH - 1, 0), (0, 0)))
    hid = conv_b + sum(pad[:, j:j + seq_len] * conv_w[j] for j in range(CONV_WIDTH))
    gate, val = jnp.split(hid, 2, axis=-1)
    return (jax.nn.silu(gate) * val) @ w_down


def setup_inputs(seed: int = 0) -> dict:
    key = jax.random.key(seed)
    ks = jax.random.split(key, 24)
    f32 = jnp.float32

    def nrm(k, shape, s):
        return jax.random.normal(k, shape, f32) * s

    def gain(k, shape):
        return 1.0 + 0.02 * jax.random.normal(k, shape, f32)

    a_im = jnp.broadcast_to(jnp.pi * jnp.arange(S5_STATE, dtype=f32), (DEPTH, S5_GROUPS, S5_STATE))
    return {
        "x": nrm(ks[0], (BATCH, SEQ, D_MODEL), 1.0),
        "in_norm_g": gain(ks[1], (DEPTH, D_MODEL)),
        "w_in": nrm(ks[2], (DEPTH, D_MODEL, IN_WIDTH), D_MODEL ** -0.5),
        "hg_lb": nrm(ks[3], (DEPTH + 1, HG_WIDTH), 0.1),
        "hg_norm_g": gain(ks[4], (DEPTH, HG_WIDTH)),
        "s5_a_re": -0.5 * jnp.exp(nrm(ks[5], (DEPTH, S5_GROUPS, S5_STATE), 0.02)),
        "s5_a_im": jnp.array(a_im),
        "s5_log_dt": jax.random.uniform(ks[6], (DEPTH, S5_GROUPS), f32, math.log(DT_MIN), math.log(DT_MAX)),
        "s5_b_re": nrm(ks[7], (DEPTH, S5_GROUPS, S5_STATE, S5_GROUP), (2 * S5_GROUP) ** -0.5),
        "s5_b_im": nrm(ks[8], (DEPTH, S5_GROUPS, S5_STATE, S5_GROUP), (2 * S5_GROUP) ** -0.5),
        "s5_c_re": nrm(ks[9], (DEPTH, S5_GROUPS, S5_GROUP, S5_STATE), S5_STATE ** -0.5),
        "s5_c_im": nrm(ks[10], (DEPTH, S5_GROUPS, S5_GROUP, S5_STATE), S5_STATE ** -0.5),
        "s5_d": nrm(ks[11], (DEPTH, S5_WIDTH), 1.0),
        "s5_w_glu": nrm(ks[12], (DEPTH, S5_WIDTH, S5_WIDTH), S5_WIDTH ** -0.5),
        "s5_b_glu": nrm(ks[13], (DEPTH, S5_WIDTH), 0.01),
        "w_out": nrm(ks[14], (DEPTH, MIX_WIDTH, D_MODEL), MIX_WIDTH ** -0.5),
        "ffn_norm_g": gain(ks[15], (DEPTH, D_MODEL)),
        "w_up": nrm(ks[16], (DEPTH, D_MODEL, 2 * D_FF), D_MODEL ** -0.5),
        "conv_w": nrm(ks[17], (DEPTH, CONV_WIDTH, 2 * D_FF), CONV_WIDTH ** -0.5),
        "conv_b": nrm(ks[18], (DEPTH, 2 * D_FF), 0.01),
        "w_down": nrm(ks[19], (DEPTH, D_FF, D_MODEL), D_FF ** -0.5),
        "final_norm_g": gain(ks[20], (D_MODEL,)),
    }


def reference(x, in_norm_g, w_in, hg_lb, hg_norm_g, s5_a_re, s5_a_im, s5_log_dt, s5_b_re, s5_b_im,
              s5_c_re, s5_c_im, s5_d, s5_w_glu, s5_b_glu, w_out, ffn_norm_g, w_up, conv_w, conv_b,
              w_down, final_norm_g):
    h = x
    lb_all = jnp.cumsum(jax.nn.softmax(hg_lb.astype(jnp.float32), axis=0), axis=0)
    for layer in range(DEPTH):
        xn = rmsnorm(h, in_norm_g[layer])
        proj = xn @ w_in[layer]
        q, fz, v, gz, u = jnp.split(proj, [HG_WIDTH, 2 * HG_WIDTH, 3 * HG_WIDTH, 4 * HG_WIDTH], axis=-1)
        o_hg = hgrn2_mix(q, fz, v, gz, lb_all[layer], hg_norm_g[layer])
        o_s5 = s5_mix(u, s5_a_re[layer], s5_a_im[layer], s5_log_dt[layer], s5_b_re[layer], s5_b_im[layer],
                      s5_c_re[layer], s5_c_im[layer], s5_d[layer], s5_w_glu[layer], s5_b_glu[layer])
        h = h + jnp.concatenate([o_hg, o_s5], axis=-1) @ w_out[layer]
        hn = rmsnorm(h, ffn_norm_g[layer])
        h = h + conv_ffn(hn, w_up[layer], conv_w[layer], conv_b[layer], w_down[layer])
    return rmsnorm(h, final_norm_g)
```

```python
import math
from contextlib import ExitStack
import numpy as np
import concourse.bass as bass
import concourse.mybir as mybir
from concourse.bass_utils import run_bass_kernel_spmd

F32 = mybir.dt.float32
BF16 = mybir.dt.bfloat16
AF = mybir.ActivationFunctionType
ALU = mybir.AluOpType

D = 1024
SEQ = 4096
HALF = 2048
DFF = 2816
NFC = 44
WT = 256
WARM = 64
NPRE = HALF // WT
NMAIN = HALF // WT
WIN = HALF + WARM + HALF
EPS = 1e-6


class Sched:
    def __init__(self, nc, es):
        self.nc, self.es = nc, es
        self.engs = {"pe": nc.tensor, "act": nc.scalar, "dve": nc.vector, "pool": nc.gpsimd, "sp": nc.sync}
        self.q = {e: [] for e in self.engs}
        self.sems, self.count = {}, {}
        self.waited = {e: {} for e in self.engs}
        self.lastw, self.readers = {}, {}
        self.cur, self.streams = None, {}
        for e in self.engs:
            self._newsem("E_" + e)

    def _newsem(self, k):
        self.sems[k] = self.es.enter_context(self.nc.semaphore(k))
        self.count[k] = 0

    def _deps(self, eng, reads, writes):
        deps = {}

        def need(sv):
            deps[sv[0]] = max(deps.get(sv[0], 0), sv[1])

        for k in reads:
            if k in self.lastw:
                need(self.lastw[k])
        for k in writes:
            if k in self.lastw:
                need(self.lastw[k])
            for r in self.readers.get(k, ()):
                need(r)
        out = []
        for sk, v in deps.items():
            if eng == "pe" and sk == "E_pe":
                continue
            if self.waited[eng].get(sk, 0) >= v:
                continue
            self.waited[eng][sk] = v
            out.append((sk, v))
        return out

    def _reg(self, sk, v, reads, writes):
        for k in reads:
            self.readers.setdefault(k, []).append((sk, v))
        for k in writes:
            self.lastw[k] = (sk, v)
            self.readers[k] = []

    def begin(self, name):
        self.cur = name
        self.streams.setdefault(name, [])

    def end(self):
        self.cur = None

    def merge(self, names, spans=None):
        items = []
        for j, n in enumerate(names):
            l = self.streams.pop(n, [])
            lo, hi = (0.0, 1.0) if spans is None else spans[j]
            for k, it in enumerate(l):
                items.append((lo + (hi - lo) * (k + 0.5) / len(l), j, k, it))
        items.sort(key=lambda t: t[:3])
        for _, _, _, (kind, a) in items:
            (self.op if kind == "op" else self.dma)(*a)

    def op(self, eng, fn, reads=(), writes=()):
        if self.cur is not None:
            self.streams[self.cur].append(("op", (eng, fn, reads, writes)))
            return
        waits = self._deps(eng, reads, writes)
        sk = "E_" + eng
        self.count[sk] += 1
        self.q[eng].append((waits, fn, sk, 1))
        self._reg(sk, self.count[sk], reads, writes)

    def dma(self, eng, out, in_, reads, writes):
        if self.cur is not None:
            self.streams[self.cur].append(("dma", (eng, out, in_, reads, writes)))
            return
        sk = "D_" + writes[0]
        if sk not in self.sems:
            self._newsem(sk)
        waits = self._deps(eng, reads, writes)
        self.count[sk] += 16
        self.q[eng].append((waits, lambda e, o=out, i=in_: e.dma_start(out=o, in_=i), sk, 16))
        self._reg(sk, self.count[sk], reads, writes)

    def barrier(self):
        for e in self.engs:
            waits = []
            for sk, c in self.count.items():
                if c > 0 and self.waited[e].get(sk, 0) < c:
                    self.waited[e][sk] = c
                    waits.append((sk, c))
            self.q[e].append((waits, None, None, 0))
        self.lastw, self.readers = {}, {}

    def finalize(self):
        need = {}
        for e in self.engs:
            for waits, fn, sk, inc in self.q[e]:
                for wk, wv in waits:
                    if wk.startswith("E_"):
                        need.setdefault(wk, set()).add(wv)
        self.rank = {}
        for wk, vs in need.items():
            for r, v in enumerate(sorted(vs)):
                self.rank[(wk, v)] = r + 1

    def replay(self, name, eng):
        n = 0
        for waits, fn, sk, inc in self.q[name]:
            for wk, wv in waits:
                eng.wait_ge(self.sems[wk], self.rank[(wk, wv)] if wk.startswith("E_") else wv)
            if fn is not None:
                ins = fn(eng)
                if inc == 16:
                    ins.then_inc(self.sems[sk], 16)
                else:
                    n += 1
                    if (sk, n) in self.rank:
                        ins.then_inc(self.sems[sk], 1)


def bc(ap, shape):
    return ap.broadcast_to(list(shape))


def build_program():
    nc = bass.Bass("TRN2", target_bir_lowering=False)
    es = ExitStack()
    with es:
        S = Sched(nc, es)

        def dram_in(name, shape):
            return nc.dram_tensor(name, list(shape), F32, kind="ExternalInput").ap()

        xT = dram_in("xT", [D, WIN])
        w_in = dram_in("w_in", [D, 2560])
        w_out = dram_in("w_out", [D, D])
        w_glu = dram_in("w_glu", [512, 512])
        w_up = dram_in("w_up", [D, 2 * DFF])
        w_down = dram_in("w_down", [DFF, D])
        smalls = dram_in("smalls", [128, 8 * 3 + 8 + 4 + 4 + 4 + 3 * NFC + NFC])
        s5s = dram_in("s5s", [128, 16 * 3])
        s5B = dram_in("s5B", [128, 2 * 16 * 16])
        s5C = dram_in("s5C", [128, 2 * 16 * 16])
        outT = nc.dram_tensor("outT", [D, HALF], F32, kind="ExternalOutput").ap()
        uscr = nc.dram_tensor("uscr", [512, WT], F32, kind="Internal").ap()
        yscr = nc.dram_tensor("yscr", [512, WT], F32, kind="Internal").ap()
        hmid = nc.dram_tensor("hmid", [D, WARM + HALF], F32, kind="Internal").ap()

        def sb(name, shape, dt=F32):
            return es.enter_context(nc.sbuf_tensor(name, list(shape), dt))

        ps = [es.enter_context(nc.psum_tensor(f"ps{i}", [128, 512], F32)) for i in range(8)]
        P = [f"ps{i}" for i in range(8)]

        sm = sb("sm", [128, 8 * 3 + 8 + 4 + 4 + 4 + 3 * NFC + NFC])
        o = 0
        g_in = sm[:, 0:8]; g_ffn = sm[:, 8:16]; g_fin = sm[:, 16:24]; o = 24
        lbraw = sm[:, o:o + 8]; o += 8
        ngv = sm[:, o:o + 4]; o += 4
        dsk = sm[:, o:o + 4]; o += 4
        bgv = sm[:, o:o + 4]; o += 4
        cw = sm[:, o:o + 3 * NFC].rearrange("p (j c) -> p j c", j=3); o += 3 * NFC
        cb = sm[:, o:o + NFC]
        cst = sb("cst", [128, 64])
        lb = cst[:, 0:4]; oml = cst[:, 4:8]; lnoml = cst[:, 8:12]; hbg = cst[:, 12:16]
        epsc = cst[:, 16:17]; halfpi = cst[:, 17:18]; tmpc = cst[:, 20:28]
        ones_b = sb("ones_b", [128, 128], BF16)
        pw = ExitStack()

        def pwsb(name, shape, dt=F32):
            return pw.enter_context(nc.sbuf_tensor(name, list(shape), dt))
        ident_f = pwsb("ident_f", [128, 128], F32)
        ident_b = pwsb("ident_b", [128, 128], BF16)
        mask2 = pwsb("mask2", [128, 64], F32)
        mreset = pwsb("mreset", [128, WT], F32)
        w_in_sb = pwsb("w_in_sb", [128, 8, 2560], BF16)
        w_out_sb = pwsb("w_out_sb", [128, 8, D], BF16)
        w_glu_sb = pwsb("w_glu_sb", [128, 4, 512], BF16)
        C8 = WT // 8
        Kt = pwsb("Kt", [128, 32, 128], BF16)
        BAre = pwsb("BAre", [128, 32, 64], BF16)
        BAim = pwsb("BAim", [128, 32, 64], BF16)
        CAre = pwsb("CAre", [128, 16, 128], BF16)
        CAimn = pwsb("CAimn", [128, 16, 128], BF16)
        rcos = pwsb("rcos", [128, 16, C8])
        rsin = pwsb("rsin", [128, 16, C8])
        R8T = pwsb("R8T", [128, 16, C8])
        r8 = pwsb("r8", [128, 16])
        xc = pwsb("xc", [128, 2, 16])
        Sst = [pwsb(f"Sst{i}", [128, 512]) for i in range(2)]
        NCH = WT // 64
        S_bf = pwsb("S_bf", [128, NCH + 1, 512], BF16)

        S.dma("sp", sm[:], smalls[:, :], [], ["sm"])
        s5sm = pwsb("s5sm", [128, 48])
        S.dma("sp", s5sm[:], s5s[:, :], [], ["s5sm"])
        S.dma("pool", w_in_sb[:], w_in.rearrange("(k p) n -> p k n", p=128), [], ["w_in_sb"])
        S.dma("pool", w_out_sb[:], w_out.rearrange("(k p) n -> p k n", p=128), [], ["w_out_sb"])
        S.dma("pool", w_glu_sb[:], w_glu.rearrange("(k p) n -> p k n", p=128), [], ["w_glu_sb"])

        S.op("pool", lambda e: e.memset(cst[:], 0.0), [], ["cst"])
        S.op("pool", lambda e: e.memset(epsc, EPS), [], ["cst"])
        S.op("pool", lambda e: e.memset(halfpi, math.pi / 2), [], ["cst"])
        S.op("pool", lambda e: e.memset(ones_b[:], 1.0), [], ["ones_b"])
        S.op("pool", lambda e: e.memset(ident_f[:], 0.0), [], ["ident_f"])
        S.op("pool", lambda e: e.affine_select(out=ident_f[:], in_=ident_f[:], pattern=[[-1, 128]], compare_op=ALU.not_equal,
                                               fill=1.0, base=0, channel_multiplier=1), ["ident_f"], ["ident_f"])
        S.op("pool", lambda e: e.tensor_copy(out=ident_b[:], in_=ident_f[:]), ["ident_f"], ["ident_b"])
        S.op("pool", lambda e: e.memset(mask2[:], 1.0), [], ["mask2"])
        S.op("pool", lambda e: e.affine_select(out=mask2[0:64, :], in_=mask2[0:64, :], pattern=[[1, 64]], compare_op=ALU.is_ge,
                                               fill=0.0, base=0, channel_multiplier=-1), ["mask2"], ["mask2"])
        S.dma("sp", mask2[64:128, :], mask2[0:64, :], ["mask2"], ["mask2"])
        S.op("pool", lambda e: e.memset(mreset[:], 1.0), [], ["mreset"])
        S.op("pool", lambda e: e.memset(mreset[:].rearrange("p (n c) -> p n c", c=64)[:, :, 0:1], 0.0), ["mreset"], ["mreset"])
        S.op("pool", lambda e: e.memset(xc[:], 0.0), [], ["xc"])
        S.op("pool", lambda e: e.memset(Sst[0][:], 0.0), [], ["Sst0"])
        S.op("pool", lambda e: e.memset(S_bf[:], 0.0), [], ["S_bf"])
        S.op("dve", lambda e: e.tensor_tensor(out=tmpc[:, 0:4], in0=lbraw[:, 4:8], in1=lbraw[:, 0:4], op=ALU.subtract), ["sm", "cst"], ["cst"])
        S.op("act", lambda e: e.activation(out=tmpc[:, 4:8], in_=tmpc[:, 0:4], func=AF.Exp), ["cst"], ["cst"])
        S.op("dve", lambda e: e.tensor_scalar(out=tmpc[:, 0:4], in0=tmpc[:, 4:8], scalar1=1.0, scalar2=None, op0=ALU.add), ["cst"], ["cst"])
        S.op("dve", lambda e: e.reciprocal(out=lb, in_=tmpc[:, 0:4]), ["cst"], ["cst"])
        S.op("dve", lambda e: e.tensor_tensor(out=oml, in0=tmpc[:, 4:8], in1=lb, op=ALU.mult), ["cst"], ["cst"])
        S.op("act", lambda e: e.activation(out=lnoml, in_=oml, func=AF.Ln), ["cst"], ["cst"])
        S.op("dve", lambda e: e.tensor_scalar(out=hbg, in0=bgv, scalar1=0.5, scalar2=None, op0=ALU.mult), ["sm", "cst"], ["cst"])

        pes = ExitStack()
        with pes:
            def tsb(name, shape, dt=F32):
                return pes.enter_context(nc.sbuf_tensor(name, list(shape), dt))
            Bsb = tsb("Bsb", [128, 2, 16, 16]); Csb = tsb("Csb", [128, 2, 16, 16])
            S.dma("sp", Bsb[:].rearrange("p a q c -> p (a q c)"), s5B[:, :], [], ["Bsb"])
            S.dma("sp", Csb[:].rearrange("p a q c -> p (a q c)"), s5C[:, :], [], ["Csb"])
            are = s5sm[:, 0:16]; aim = s5sm[:, 16:32]; ldt = s5sm[:, 32:48]
            t16 = tsb("t16", [128, 12, 16])
            tv = lambda i: t16[:, i, :]
            K16 = ["t16"]

            def dv(fn):
                S.op("dve", fn, K16 + ["s5sm"], K16)

            def ac(fn):
                S.op("act", fn, K16 + ["s5sm", "cst"], K16)
            DT, DAR, DAI, MAG, CS, SN, T0, T1, T2, ZR, ZI, DEN = range(12)
            ac(lambda e: e.activation(out=tv(DT), in_=ldt, func=AF.Exp))
            dv(lambda e: e.tensor_tensor(out=tv(DAR), in0=tv(DT), in1=are, op=ALU.mult))
            dv(lambda e: e.tensor_tensor(out=tv(DAI), in0=tv(DT), in1=aim, op=ALU.mult))
            ac(lambda e: e.activation(out=tv(MAG), in_=tv(DAR), func=AF.Exp))
            ac(lambda e: e.activation(out=tv(SN), in_=tv(DAI), func=AF.Sin, scale=1.0 / 64))
            ac(lambda e: e.activation(out=tv(CS), in_=tv(DAI), func=AF.Sin, scale=1.0 / 64, bias=halfpi))
            for _ in range(6):
                dv(lambda e: e.tensor_tensor(out=tv(T0), in0=tv(CS), in1=tv(CS), op=ALU.mult))
                dv(lambda e: e.tensor_tensor(out=tv(T1), in0=tv(SN), in1=tv(SN), op=ALU.mult))
                dv(lambda e: e.scalar_tensor_tensor(out=tv(SN), in0=tv(SN), scalar=2.0, in1=tv(CS), op0=ALU.mult, op1=ALU.mult))
                dv(lambda e: e.tensor_tensor(out=tv(CS), in0=tv(T0), in1=tv(T1), op=ALU.subtract))
            TPr = tsb("TPr", [128, 16, 17]); TPi = tsb("TPi", [128, 16, 17])
            KT = ["TP", "t16"]

            def dt_(fn):
                S.op("dve", fn, KT, KT)
            dt_(lambda e: e.memset(TPr[:, :, 8:9], 1.0))
            dt_(lambda e: e.memset(TPi[:, :, 8:9], 0.0))
            dt_(lambda e: e.tensor_tensor(out=TPr[:, :, 9], in0=tv(MAG), in1=tv(CS), op=ALU.mult))
            dt_(lambda e: e.tensor_tensor(out=TPi[:, :, 9], in0=tv(MAG), in1=tv(SN), op=ALU.mult))
            dt_(lambda e: e.tensor_tensor(out=tv(T0), in0=tv(MAG), in1=tv(MAG), op=ALU.mult))
            dt_(lambda e: e.reciprocal(out=tv(T1), in_=tv(T0)))
            dt_(lambda e: e.tensor_tensor(out=TPr[:, :, 7], in0=TPr[:, :, 9], in1=tv(T1), op=ALU.mult))
            dt_(lambda e: e.scalar_tensor_tensor(out=TPi[:, :, 7], in0=TPi[:, :, 9], scalar=-1.0, in1=tv(T1), op0=ALU.mult, op1=ALU.mult))
            cmt = tsb("cmt", [128, 4, 16, 16])

            def cmul(key, outr, outi, ar, ai, br, bi, shape, eng="dve"):
                n = 1
                for s_ in shape:
                    n *= s_
                tt = [cmt[:, i].rearrange("p a b -> p (a b)")[:, 0:n] for i in range(4)]
                if len(shape) == 2:
                    tt = [t.rearrange("p (a b) -> p a b", a=shape[0]) for t in tt]
                k = key + ["cmt"]
                S.op(eng, lambda e: e.tensor_tensor(out=tt[0], in0=ar, in1=br, op=ALU.mult), k, k)
                S.op(eng, lambda e: e.tensor_tensor(out=tt[1], in0=ai, in1=bi, op=ALU.mult), k, k)
                S.op(eng, lambda e: e.tensor_tensor(out=tt[2], in0=ar, in1=bi, op=ALU.mult), k, k)
                S.op(eng, lambda e: e.tensor_tensor(out=tt[3], in0=ai, in1=br, op=ALU.mult), k, k)
                S.op(eng, lambda e: e.tensor_tensor(out=outr, in0=tt[0], in1=tt[1], op=ALU.subtract), k, k)
                S.op(eng, lambda e: e.tensor_tensor(out=outi, in0=tt[2], in1=tt[3], op=ALU.add), k, k)
            for (lo, n, step) in ((9, 1, 1), (9, 2, 2), (9, 4, 4)):
                cmul(KT, TPr[:, :, lo + step:lo + step + n], TPi[:, :, lo + step:lo + step + n],
                     TPr[:, :, lo:lo + n], TPi[:, :, lo:lo + n],
                     bc(TPr[:, :, 8 + step:9 + step], [128, 16, n]), bc(TPi[:, :, 8 + step:9 + step], [128, 16, n]), [16, n])
            for (lo, n, step) in ((7, 1, 1), (6, 2, 2), (4, 4, 4)):
                cmul(KT, TPr[:, :, lo - step:lo - step + n], TPi[:, :, lo - step:lo - step + n],
                     TPr[:, :, lo:lo + n], TPi[:, :, lo:lo + n],
                     bc(TPr[:, :, 8 - step:9 - step], [128, 16, n]), bc(TPi[:, :, 8 - step:9 - step], [128, 16, n]), [16, n])
            dt_(lambda e: e.tensor_scalar(out=tv(T0), in0=TPr[:, :, 9], scalar1=-1.0, scalar2=None, op0=ALU.add))
            S.op("dve", lambda e: e.tensor_tensor(out=tv(T1), in0=are, in1=are, op=ALU.mult), KT + ["s5sm"], KT)
            S.op("dve", lambda e: e.tensor_tensor(out=tv(T2), in0=aim, in1=aim, op=ALU.mult), KT + ["s5sm"], KT)
            dt_(lambda e: e.tensor_tensor(out=tv(DEN), in0=tv(T1), in1=tv(T2), op=ALU.add))
            dt_(lambda e: e.reciprocal(out=tv(DEN), in_=tv(DEN)))
            S.op("dve", lambda e: e.tensor_scalar(out=tv(T2), in0=aim, scalar1=-1.0, scalar2=None, op0=ALU.mult), KT + ["s5sm"], KT)
            S.op("dve", lambda e: e.tensor_copy(out=tv(T1), in_=are), KT + ["s5sm"], KT)
            cmul(KT, tv(ZR), tv(ZI), tv(T0), TPi[:, :, 9], tv(T1), tv(T2), [16])
            dt_(lambda e: e.tensor_tensor(out=tv(ZR), in0=tv(ZR), in1=tv(DEN), op=ALU.mult))
            dt_(lambda e: e.tensor_tensor(out=tv(ZI), in0=tv(ZI), in1=tv(DEN), op=ALU.mult))
            cRr = tsb("cRr", [128, 16, 8]); cRi = tsb("cRi", [128, 16, 8])
            KT2 = KT + ["cR"]
            for s_ in range(8):
                cmul(KT2, cRr[:, :, s_], cRi[:, :, s_], TPr[:, :, 15 - s_], TPi[:, :, 15 - s_], tv(ZR), tv(ZI), [16])
            Rre = tsb("Rre", [128, 16, 8, 16]); Rim = tsb("Rim", [128, 16, 8, 16])
            Qre = tsb("Qre", [128, 16, 8, 16]); Qimn = tsb("Qimn", [128, 16, 8, 16])
            big = tsb("bigt", [128, 2, 16, 8, 16])

            def cmul_big(key, outr, outi, cr, ci, br, bi, neg_im=False):
                crb = bc(cr.unsqueeze(3), [128, 16, 8, 16]); cib = bc(ci.unsqueeze(3), [128, 16, 8, 16])
                brb = bc(br.unsqueeze(2), [128, 16, 8, 16]); bib = bc(bi.unsqueeze(2), [128, 16, 8, 16])
                k = key + ["bigt"]
                S.op("dve", lambda e: e.tensor_tensor(out=big[:, 0], in0=crb, in1=brb, op=ALU.mult), k, k)
                S.op("dve", lambda e: e.tensor_tensor(out=big[:, 1], in0=cib, in1=bib, op=ALU.mult), k, k)
                S.op("dve", lambda e: e.tensor_tensor(out=outr, in0=big[:, 0], in1=big[:, 1], op=ALU.subtract), k, k)
                S.op("dve", lambda e: e.tensor_tensor(out=big[:, 0], in0=crb, in1=bib, op=ALU.mult), k, k)
                S.op("dve", lambda e: e.tensor_tensor(out=big[:, 1], in0=cib, in1=brb, op=ALU.mult), k, k)
                if neg_im:
                    S.op("dve", lambda e: e.scalar_tensor_tensor(out=outi, in0=big[:, 0], scalar=-1.0, in1=big[:, 1], op0=ALU.mult, op1=ALU.subtract), k, k)
                else:
                    S.op("dve", lambda e: e.tensor_tensor(out=outi, in0=big[:, 0], in1=big[:, 1], op=ALU.add), k, k)
            KB = KT2 + ["Bsb", "Csb", "RQ"]
            cmul_big(KB, Rre[:], Rim[:], cRr[:], cRi[:], Bsb[:, 0], Bsb[:, 1])
            cmul_big(KB, Qre[:], Qimn[:], TPr[:, :, 1:9], TPi[:, :, 1:9], Csb[:, 0], Csb[:, 1], neg_im=True)
            mk = tsb("mk", [128, 8, 16])
            S.op("pool", lambda e: e.memset(mk[:], 1.0), [], ["mk"])
            S.op("pool", lambda e: e.affine_select(out=mk[:], in_=mk[:], pattern=[[16, 8], [0, 16]], compare_op=ALU.is_ge, fill=0.0,
                                                   base=15, channel_multiplier=-1), ["mk"], ["mk"])
            for g in range(32):
                q_, gi = g // 2, g % 2
                rs = slice(gi * 64, gi * 64 + 64)
                pt = ps[g % 4]
                S.op("pe", lambda e, q_=q_, rs=rs, pt=pt: e.matmul(pt[:, 0:128], Rre[rs, q_].rearrange("p a b -> p (a b)"),
                                                                    Qre[rs, q_].rearrange("p a b -> p (a b)"), start=True, stop=False), KB, [P[g % 4]])
                S.op("pe", lambda e, q_=q_, rs=rs, pt=pt: e.matmul(pt[:, 0:128], Rim[rs, q_].rearrange("p a b -> p (a b)"),
                                                                    Qimn[rs, q_].rearrange("p a b -> p (a b)"), start=False, stop=True), KB, [P[g % 4]])
                S.op("dve", lambda e, g=g, pt=pt: e.tensor_tensor(out=Kt[:, g, :], in0=pt[:, 0:128], in1=mk[:].rearrange("p a b -> p (a b)"), op=ALU.mult),
                     [P[g % 4], "mk"], ["Kt"])
                for (Rt, BAt, nm) in ((Rre, BAre, "BAre"), (Rim, BAim, "BAim")):
                    pt2 = ps[4 + (g % 2) * 2 + (0 if nm == "BAre" else 1)]
                    pk = P[4 + (g % 2) * 2 + (0 if nm == "BAre" else 1)]
                    S.op("pe", lambda e, Rt=Rt, pt2=pt2, q_=q_, rs=rs: e.transpose(pt2[:, 0:64], Rt[rs, q_].rearrange("p a b -> p (a b)"), ident_f[rs, rs]),
                         KB + ["ident_f"], [pk])
                    S.op("act", lambda e, BAt=BAt, pt2=pt2, g=g: e.activation(out=BAt[:, g, :], in_=pt2[:, 0:64], func=AF.Copy), [pk], [nm])
            cmul_big(KB, Qre[:], Qimn[:], TPr[:, :, 9:17], TPi[:, :, 9:17], Csb[:, 0], Csb[:, 1], neg_im=True)
            S.op("act", lambda e: e.activation(out=CAre[:].rearrange("p q m -> p (q m)"), in_=Qre[:].rearrange("p q a b -> p (q a b)"), func=AF.Copy), KB, ["CAre"])
            S.op("act", lambda e: e.activation(out=CAimn[:].rearrange("p q m -> p (q m)"), in_=Qimn[:].rearrange("p q a b -> p (q a b)"), func=AF.Copy), KB, ["CAimn"])
            S.op("act", lambda e: e.activation(out=r8[:], in_=tv(DAR), func=AF.Exp, scale=8.0), KT, ["r8"])
            S.op("dve", lambda e: e.reciprocal(out=tv(T0), in_=r8[:]), KT + ["r8"], KT)
            S.op("dve", lambda e: e.tensor_tensor(out=rcos[:, :, 0], in0=TPr[:, :, 16], in1=tv(T0), op=ALU.mult), KT + ["r8"], ["rot"])
            S.op("dve", lambda e: e.tensor_tensor(out=rsin[:, :, 0], in0=TPi[:, :, 16], in1=tv(T0), op=ALU.mult), KT, ["rot"])
            n_ = 1
            while n_ < C8:
                cmul(["rot"], rcos[:, :, n_:2 * n_], rsin[:, :, n_:2 * n_], rcos[:, :, 0:n_], rsin[:, :, 0:n_],
                     bc(rcos[:, :, n_ - 1:n_], [128, 16, n_]), bc(rsin[:, :, n_ - 1:n_], [128, 16, n_]), [16, n_])
                n_ *= 2
            S.op("dve", lambda e: e.tensor_copy(out=R8T[:], in_=bc(r8[:].unsqueeze(2), [128, 16, C8])), ["r8"], ["R8T"])
            S.op("dve", lambda e: e.memset(R8T[:, :, 0:1], 0.0), ["R8T"], ["R8T"])
            S.barrier()
        p1 = ExitStack()
        with p1:
            def wsb(name, shape, dt=F32):
                return p1.enter_context(nc.sbuf_tensor(name, list(shape), dt))
            xt = wsb("xt", [128, 8, WT])
            xsq = wsb("xsq", [128, 8, WT], BF16)
            rstd = wsb("rstd", [128, 2, WT])
            XN = [wsb(f"xn{i}", [128, 8, WT], BF16) for i in range(2)]
            MIX = [wsb(f"mix{i}", [128, 8, WT], BF16) for i in range(2)]
            HMs = wsb("HMs", [128, 8, WT])
            Fs = [wsb(f"F{i}", [128, 4, WT]) for i in range(10)]
            ktT = wsb("ktT", [128, 4, WT], BF16)
            y2b = wsb("y2b", [128, 4, WT], BF16)
            kdecT = wsb("kdecT", [128, 4, WT], BF16)
            qdecT = wsb("qdecT", [128, 4, WT], BF16)
            ktm = wsb("ktm", [128, WT // 128, 512], BF16)
            vtm = wsb("vtm", [128, WT // 128, 512], BF16)
            attm = wsb("attm", [128, 2, 256], BF16)
            Utf = wsb("Utf", [128, 32, C8])
            Utfb = wsb("Utfb", [128, 32, C8], BF16)
            Xin = wsb("Xin", [128, 2, 16, C8], BF16)
            xfull = wsb("xfull", [128, 2, 16, C8 + 1])

            tiles = [(i * WT, WT, "pre") for i in range(NPRE)] + [(HALF, WARM, "warm")] + [(HALF + WARM + i * WT, WT, "main") for i in range(NMAIN)]
            scur = [0]
            hh = lambda h, W: (h // 2, slice((h % 2) * 256, (h % 2) * 256 + W))

            def xload(ti):
                t0, W, kind = tiles[ti]
                S.dma("pool", xt[:, :, 0:W], xT.rearrange("(k p) t -> p k t", p=128)[:, :, t0:t0 + W], [], ["xt"])

            def pre(ti, staged=False, do_load=True):
                t0, W, kind = tiles[ti]
                xn, xk = XN[ti % 2], f"xn{ti % 2}"
                if do_load:
                    xload(ti)
                if staged:
                    S.begin("Jb")
                S.op("act", lambda e: e.activation(out=xsq[:, :, 0:W], in_=xt[:, :, 0:W], func=AF.Square), ["xt"], ["xsq"])
                for k in range(8):
                    S.op("pe", lambda e, k=k: e.matmul(ps[7][:, 0:W], ones_b[:], xsq[:, k, 0:W], start=(k == 0), stop=(k == 7)), ["xsq", "ones_b"], [P[7]])
                S.op("act", lambda e: e.activation(out=rstd[:, 1, 0:W], in_=ps[7][:, 0:W], func=AF.Ln, scale=1.0 / D, bias=epsc), [P[7], "cst"], ["rstd1"])
                S.op("act", lambda e: e.activation(out=rstd[:, 0, 0:W], in_=rstd[:, 1, 0:W], func=AF.Exp, scale=-0.5), ["rstd1"], ["rstd0"])
                for k in range(8):
                    S.op("dve", lambda e, k=k: e.scalar_tensor_tensor(out=xn[:, k, 0:W], in0=xt[:, k, 0:W], scalar=g_in[:, k:k + 1], in1=rstd[:, 0, 0:W],
                                                                       op0=ALU.mult, op1=ALU.mult), ["xt", "rstd0", "sm"], [xk])

            def proj(xn, xk, col0, W, pt, pk, perm=False):
                for k in range(8):
                    rhs = xn[:, k, 0:W]
                    if perm:
                        rhs = rhs.rearrange("p (c s) -> p s c", s=8)
                    S.op("pe", lambda e, k=k, rhs=rhs: e.matmul(pt, w_in_sb[:, k, col0:col0 + 128], rhs, start=(k == 0), stop=(k == 7)), [xk, "w_in_sb"], [pk])

            def hstream(ti):
                t0, W, kind = tiles[ti]
                full = kind != "pre"
                nch = W // 64
                nsub = max(1, W // 128)
                subw = min(W, 128)
                xn, xk = XN[ti % 2], f"xn{ti % 2}"
                mix, mka = MIX[ti % 2], f"mixa{ti % 2}"
                E1, L1, L2, B1, SG = Fs[0], Fs[1], Fs[2], Fs[3], Fs[4]
                zz = lambda h: ps[h // 2][:, (h % 2) * 256:(h % 2) * 256 + W]
                for h in range(4):
                    proj(xn, xk, 512 + h * 128, W, zz(h), P[h // 2])
                for h in range(4):
                    S.op("act", lambda e, h=h: e.activation(out=E1[:, h, 0:W], in_=zz(h), func=AF.Exp, scale=-1.0), [P[h // 2]], ["F0"])
                for h in range(4):
                    S.op("act", lambda e, h=h: e.activation(out=L1[:, h, 0:W], in_=E1[:, h, 0:W], func=AF.Ln, scale=lb[:, h:h + 1], bias=1.0), ["F0", "cst"], ["F1"])
                S.op("act", lambda e: e.activation(out=L2[:, :, 0:W], in_=E1[:, :, 0:W], func=AF.Ln, scale=1.0, bias=1.0), ["F0"], ["F2"])
                S.op("dve", lambda e: e.tensor_tensor(out=L1[:, :, 0:W], in0=L1[:, :, 0:W], in1=L2[:, :, 0:W], op=ALU.subtract), ["F1", "F2"], ["F1"])
                for h in range(4):
                    S.op("dve", lambda e, h=h: e.tensor_tensor_scan(out=B1[:, h, 0:W], data0=mreset[:, 0:W], data1=L1[:, h, 0:W], initial=0.0,
                                                                     op0=ALU.mult, op1=ALU.add), ["F1", "mreset"], ["F3"])
                for h in range(4):
                    S.op("dve", lambda e, h=h: e.tensor_tensor(out=L2[:, h, 0:W], in0=zz(h), in1=L2[:, h, 0:W], op=ALU.add), [P[h // 2], "F2"], ["F2"])
                S.op("dve", lambda e: e.tensor_tensor(out=L2[:, :, 0:W], in0=L2[:, :, 0:W], in1=B1[:, :, 0:W], op=ALU.add), ["F2", "F3"], ["F2"])
                S.op("act", lambda e: e.activation(out=E1[:, :, 0:W], in_=B1[:, :, 0:W], func=AF.Exp), ["F3"], ["F0"])
                for h in range(4):
                    S.op("act", lambda e, h=h: e.activation(out=kdecT[:, h, 0:W], in_=L2[:, h, 0:W], func=AF.Exp, scale=-1.0, bias=lnoml[:, h:h + 1]),
                         ["F2", "cst"], ["kdecT"])
                for h in range(4):
                    S.op("dve", lambda e, h=h: e.tensor_tensor(out=ktT[:, h, 0:W].rearrange("p (n c) -> p n c", c=64),
                                                               in0=kdecT[:, h, 0:W].rearrange("p (n c) -> p n c", c=64),
                                                               in1=bc(E1[:, h, 0:W].rearrange("p (n c) -> p n c", c=64)[:, :, 63:64], [128, nch, 64]),
                                                               op=ALU.mult), ["kdecT", "F0"], ["ktT"])
                for s_ in range(nsub):
                    pt = ps[2 + s_]
                    for k in range(8):
                        S.op("pe", lambda e, k=k, s_=s_, pt=pt: e.matmul(pt[0:subw, 0:512], xn[:, k, s_ * 128:s_ * 128 + subw], w_in_sb[:, k, 1024:1536],
                                                                          start=(k == 0), stop=(k == 7)), [xk, "w_in_sb"], [P[2 + s_]])
                    S.op("act", lambda e, s_=s_, pt=pt: e.activation(out=vtm[0:subw, s_, :], in_=pt[0:subw, 0:512], func=AF.Copy), [P[2 + s_]], ["vtm"])
                for s_ in range(nsub):
                    ptb = ps[2 + s_][:].bitcast(BF16)
                    for h in range(4):
                        S.op("pe", lambda e, s_=s_, h=h, ptb=ptb: e.transpose(ptb[0:subw, h * 128:(h + 1) * 128], ktT[:, h, s_ * 128:s_ * 128 + subw], ident_b[:]),
                             ["ktT", "ident_b"], [P[2 + s_]])
                    S.op("act", lambda e, s_=s_, ptb=ptb: e.activation(out=ktm[0:subw, s_, :], in_=ptb[0:subw, 0:512], func=AF.Copy), [P[2 + s_]], ["ktm"])
                for n in range(nch):
                    s_, half = n // 2, (n % 2) * 64
                    pk = 2 + (n % 2)
                    for h in range(4):
                        S.op("pe", lambda e, n=n, h=h, s_=s_, half=half, pk=pk: e.matmul(ps[pk][:, h * 128:(h + 1) * 128], ktm[half:half + 64, s_, h * 128:(h + 1) * 128],
                                                                                        vtm[half:half + 64, s_, h * 128:(h + 1) * 128], start=True, stop=True),
                             ["ktm", "vtm"], [P[pk]])
                    So, Sn = Sst[scur[0]], Sst[1 - scur[0]]
                    ko, kn = f"Sst{scur[0]}", f"Sst{1 - scur[0]}"
                    S.op("dve", lambda e, n=n, So=So, Sn=Sn: e.tensor_tensor(out=Sn[:].rearrange("p (h v) -> p h v", h=4), in0=So[:].rearrange("p (h v) -> p h v", h=4),
                                                                             in1=bc(E1[:, :, n * 64 + 63:n * 64 + 64], [128, 4, 128]), op=ALU.mult), [ko, "F0"], [kn])
                    S.op("dve", lambda e, Sn=Sn, pk=pk: e.tensor_tensor(out=Sn[:], in0=ps[pk][:, :], in1=Sn[:], op=ALU.add), [kn, P[pk]], [kn])
                    scur[0] = 1 - scur[0]
                    if full or n == nch - 1:
                        S.op("act", lambda e, Sn=Sn, dst=n + 1: e.activation(out=S_bf[:, dst, :], in_=Sn[:], func=AF.Copy), [kn], ["S_bf"])
                if not full:
                    S.op("act", lambda e: e.activation(out=S_bf[:, 0, :], in_=S_bf[:, nch, :], func=AF.Copy), ["S_bf"], ["S_bf"])
                    return
                for h in range(4):
                    proj(xn, xk, h * 128, W, zz(h), P[h // 2])
                    S.op("dve", lambda e, h=h: e.tensor_tensor(out=qdecT[:, h, 0:W], in0=zz(h), in1=E1[:, h, 0:W], op=ALU.mult), [P[h // 2], "F0"], ["qdecT"])
                gg = lambda h: ps[2 + h // 2][:, (h % 2) * 256:(h % 2) * 256 + W]
                for h in range(4):
                    proj(xn, xk, 1536 + h * 128, W, gg(h), P[2 + h // 2])
                    S.op("act", lambda e, h=h: e.activation(out=SG[:, h, 0:W], in_=gg(h), func=AF.Silu), [P[2 + h // 2]], ["F4"])
                oo = lambda h, n: ps[2 + h // 2][:, (h % 2) * 256 + n * 64:(h % 2) * 256 + (n + 1) * 64]
                for pr in range((nch + 1) // 2):
                    for n in range(2 * pr, min(2 * pr + 2, nch)):
                        half = (n % 2) * 64
                        for h in range(4):
                            S.op("pe", lambda e, n=n, h=h, half=half: e.matmul(ps[0][half:half + 64, h * 64:(h + 1) * 64], kdecT[:, h, n * 64:(n + 1) * 64],
                                                                              qdecT[:, h, n * 64:(n + 1) * 64], start=True, stop=True), ["kdecT", "qdecT"], [P[0]])
                    np_ = 128 if 2 * pr + 1 < nch else 64
                    S.op("dve", lambda e, pr=pr, np_=np_: e.tensor_tensor(out=attm[0:np_, pr % 2, :].rearrange("p (h c) -> p h c", h=4),
                                                                          in0=ps[0][0:np_, 0:256].rearrange("p (h c) -> p h c", h=4),
                                                                          in1=bc(mask2[0:np_, :].unsqueeze(1), [np_, 4, 64]), op=ALU.mult), [P[0], "mask2"], [f"attm{pr % 2}"])
                    for n in range(2 * pr, min(2 * pr + 2, nch)):
                        half = (n % 2) * 64
                        s_ = n // 2
                        for h in range(4):
                            po = oo(h, n)
                            S.op("pe", lambda e, n=n, h=h, po=po: e.matmul(po, S_bf[:, n, h * 128:(h + 1) * 128], qdecT[:, h, n * 64:(n + 1) * 64], start=True, stop=False),
                                 ["S_bf", "qdecT"], [P[2 + h // 2]])
                            S.op("pe", lambda e, n=n, h=h, po=po, half=half, s_=s_, pr=pr: e.matmul(po, vtm[half:half + 64, s_, h * 128:(h + 1) * 128],
                                                                                                 attm[half:half + 64, pr % 2, h * 64:(h + 1) * 64], start=False, stop=True),
                                 ["vtm", f"attm{pr % 2}"], [P[2 + h // 2]])
                S.op("act", lambda e: e.activation(out=S_bf[:, 0, :], in_=S_bf[:, nch, :], func=AF.Copy), ["S_bf"], ["S_bf"])
                ov = lambda h: ps[2 + h // 2][:, (h % 2) * 256:(h % 2) * 256 + W]
                for h in range(4):
                    S.op("act", lambda e, h=h: e.activation(out=ktT[:, h, 0:W], in_=ov(h), func=AF.Square), [P[2 + h // 2]], ["ktT"])
                OT = Fs[1]
                for h in range(4):
                    pk = h % 2
                    S.op("pe", lambda e, h=h, pk=pk: e.matmul(ps[pk][:, 0:W], ones_b[:], ktT[:, h, 0:W], start=True, stop=True), ["ktT", "ones_b"], [P[pk]])
                    S.op("act", lambda e, h=h, pk=pk: e.activation(out=OT[:, h, 0:W], in_=ps[pk][:, 0:W], func=AF.Ln, scale=1.0 / 128, bias=epsc), [P[pk], "cst"], ["F1"])
                S.op("act", lambda e: e.activation(out=OT[:, :, 0:W], in_=OT[:, :, 0:W], func=AF.Exp, scale=-0.5), ["F1"], ["F1"])
                for h in range(4):
                    S.op("dve", lambda e, h=h: e.scalar_tensor_tensor(out=OT[:, h, 0:W], in0=ov(h), scalar=ngv[:, h:h + 1], in1=OT[:, h, 0:W],
                                                                       op0=ALU.mult, op1=ALU.mult), [P[2 + h // 2], "F1", "sm"], ["F1"])
                S.op("dve", lambda e: e.tensor_tensor(out=mix[:, 0:4, 0:W], in0=OT[:, :, 0:W], in1=SG[:, :, 0:W], op=ALU.mult), ["F1", "F4"], [mka])

            def sstream(ti, n2, n3):
                t0, W, kind = tiles[ti]
                full = kind != "pre"
                C = W // 8
                xn, xk = XN[ti % 2], f"xn{ti % 2}"
                mix, mkb = MIX[ti % 2], f"mixb{ti % 2}"
                Usb, uk = (Fs[5], "F5") if ti % 2 == 0 else (Fs[9], "F9")
                uu = lambda kk: ps[6 + kk // 2][:, (kk % 2) * 256:(kk % 2) * 256 + W]
                for kk in range(4):
                    proj(xn, xk, 2048 + kk * 128, W, uu(kk), P[6 + kk // 2], perm=True)
                for kk in range(4):
                    S.op("act", lambda e, kk=kk: e.activation(out=Usb[:, kk, 0:W], in_=uu(kk), func=AF.Copy), [P[6 + kk // 2]], [uk])
                S.end(); S.begin("S1xb")
                S.dma("sp", uscr.rearrange("(k p) t -> p k t", p=128)[:, :, 0:W], Usb[:, :, 0:W], [uk], ["uscr"])
                uv = uscr[:, 0:W].rearrange("(g p) (s c) -> p g s c", p=16, s=8)
                for s_ in range(8):
                    S.dma("sp", Utf[s_ * 16:(s_ + 1) * 16, :, 0:C], uv[:, :, s_, :], ["uscr"], ["Utf"])
                S.end(); S.begin(n2 + "a")
                S.op("pool", lambda e: e.tensor_copy(out=Utfb[:, :, 0:C], in_=Utf[:, :, 0:C]), ["Utf"], ["Utfb"])
                S.end(); S.begin(n2 + "b")
                for g in range(32):
                    q_, gi = g // 2, g % 2
                    for (BAt, pk, nm) in ((BAre, 4, "BAre"), (BAim, 5, "BAim")):
                        S.op("pe", lambda e, g=g, q_=q_, gi=gi, pk=pk, BAt=BAt: e.matmul(ps[pk][gi * 64:gi * 64 + 64, q_ * C:(q_ + 1) * C], BAt[:, g, :], Utfb[:, g, 0:C],
                                                                                        start=True, stop=True), ["Utfb", nm], [P[pk]])
                BR, VR, XR = Fs[6], Fs[7], Fs[8]
                v4 = lambda t: t[:].rearrange("p a w -> p (a w)").rearrange("p (a q c) -> p a q c", a=2, q=16)
                brv, vrv, tmv = v4(BR), v4(VR), v4(XR)
                Lre = ps[4][:, 0:16 * C].rearrange("p (q c) -> p q c", q=16)
                Lim = ps[5][:, 0:16 * C].rearrange("p (q c) -> p q c", q=16)
                S.op("dve", lambda e: e.tensor_tensor(out=brv[:, 0, :, 0:C], in0=Lre, in1=rcos[:, :, 0:C], op=ALU.mult), [P[4], "rot"], ["F6"])
                S.op("dve", lambda e: e.tensor_tensor(out=tmv[:, 0, :, 0:C], in0=Lim, in1=rsin[:, :, 0:C], op=ALU.mult), [P[5], "rot"], ["F8"])
                S.op("dve", lambda e: e.tensor_tensor(out=brv[:, 1, :, 0:C], in0=Lim, in1=rcos[:, :, 0:C], op=ALU.mult), [P[5], "rot"], ["F6"])
                S.op("dve", lambda e: e.tensor_tensor(out=tmv[:, 1, :, 0:C], in0=Lre, in1=rsin[:, :, 0:C], op=ALU.mult), [P[4], "rot"], ["F8"])
                S.op("pool", lambda e: e.tensor_tensor(out=brv[:, 0, :, 0:C], in0=brv[:, 0, :, 0:C], in1=tmv[:, 0, :, 0:C], op=ALU.add), ["F6", "F8"], ["F6"])
                S.op("pool", lambda e: e.tensor_tensor(out=brv[:, 1, :, 0:C], in0=brv[:, 1, :, 0:C], in1=tmv[:, 1, :, 0:C], op=ALU.subtract), ["F6", "F8"], ["F6"])
                S.op("pool", lambda e: e.tensor_tensor(out=tmv[:, :, :, 0], in0=xc[:], in1=bc(r8[:].unsqueeze(1), [128, 2, 16]), op=ALU.mult), ["xc", "r8", "F8"], ["F8"])
                S.op("pool", lambda e: e.tensor_tensor(out=brv[:, :, :, 0], in0=brv[:, :, :, 0], in1=tmv[:, :, :, 0], op=ALU.add), ["F6", "F8"], ["F6"])
                for a_ in range(2):
                    if C == C8:
                        d0 = R8T[:].rearrange("p q c -> p (q c)"); d1 = brv[:, a_].rearrange("p q c -> p (q c)"); oo_ = vrv[:, a_].rearrange("p q c -> p (q c)")
                        S.op("dve", lambda e, d0=d0, d1=d1, oo_=oo_: e.tensor_tensor_scan(out=oo_, data0=d0, data1=d1, initial=0.0, op0=ALU.mult, op1=ALU.add),
                             ["F6", "R8T"], ["F7"])
                    else:
                        for q_ in range(16):
                            S.op("dve", lambda e, a_=a_, q_=q_: e.tensor_tensor_scan(out=vrv[:, a_, q_, 0:C], data0=R8T[:, q_, 0:C], data1=brv[:, a_, q_, 0:C], initial=0.0,
                                                                                     op0=ALU.mult, op1=ALU.add), ["F6", "R8T"], ["F7"])
                S.op("pool", lambda e: e.tensor_copy(out=xfull[:, :, :, 0], in_=xc[:]), ["xc"], ["xfull"])
                S.op("dve", lambda e: e.tensor_tensor(out=tmv[:, 0, :, 0:C], in0=vrv[:, 1, :, 0:C], in1=rsin[:, :, 0:C], op=ALU.mult), ["F7", "rot", "F8"], ["F8"])
                S.op("pool", lambda e: e.tensor_tensor(out=tmv[:, 1, :, 0:C], in0=vrv[:, 0, :, 0:C], in1=rsin[:, :, 0:C], op=ALU.mult), ["F7", "rot", "F8"], ["F8"])
                S.op("dve", lambda e: e.tensor_tensor(out=brv[:, 0, :, 0:C], in0=vrv[:, 0, :, 0:C], in1=rcos[:, :, 0:C], op=ALU.mult), ["F7", "rot", "F6"], ["F6"])
                S.op("pool", lambda e: e.tensor_tensor(out=brv[:, 1, :, 0:C], in0=vrv[:, 1, :, 0:C], in1=rcos[:, :, 0:C], op=ALU.mult), ["F7", "rot", "F6"], ["F6"])
                S.op("dve", lambda e: e.tensor_tensor(out=xfull[:, 0, :, 1:C + 1], in0=brv[:, 0, :, 0:C], in1=tmv[:, 0, :, 0:C], op=ALU.subtract), ["F6", "F8", "xfull"], ["xfull"])
                S.op("pool", lambda e: e.tensor_tensor(out=xfull[:, 1, :, 1:C + 1], in0=brv[:, 1, :, 0:C], in1=tmv[:, 1, :, 0:C], op=ALU.add), ["F6", "F8", "xfull"], ["xfull"])
                S.op("pool", lambda e: e.tensor_copy(out=xc[:], in_=xfull[:, :, :, C]), ["xfull"], ["xc"])
                if not full:
                    return
                S.end(); S.begin(n2 + "c")
                S.op("pool", lambda e: e.tensor_copy(out=Xin[:, :, :, 0:C], in_=xfull[:, :, :, 0:C]), ["xfull"], ["Xin"])
                for g in range(32):
                    q_, gi = g // 2, g % 2
                    rs = slice(gi * 64, gi * 64 + 64)
                    pk = 4 + g // 16
                    po = ps[pk][:, (g % 16) * C:(g % 16 + 1) * C]
                    S.op("pe", lambda e, g=g, po=po: e.matmul(po, Kt[:, g, :], Utfb[:, g, 0:C], start=True, stop=False), ["Kt", "Utfb"], [P[pk]])
                    S.op("pe", lambda e, q_=q_, rs=rs, po=po: e.matmul(po, CAre[rs, q_, :], Xin[rs, 0, q_, 0:C], start=False, stop=False), ["CAre", "Xin"], [P[pk]])
                    S.op("pe", lambda e, q_=q_, rs=rs, po=po: e.matmul(po, CAimn[rs, q_, :], Xin[rs, 1, q_, 0:C], start=False, stop=True), ["CAimn", "Xin"], [P[pk]])
                Ytf = Fs[6]
                ytv = Ytf[:].rearrange("p a w -> p (a w)")[:, 0:32 * C].rearrange("p (g c) -> p g c", g=32)
                for hb in range(2):
                    S.op("act", lambda e, hb=hb: e.activation(out=ytv[:, hb * 16:(hb + 1) * 16, :], in_=ps[4 + hb][:, 0:16 * C].rearrange("p (g c) -> p g c", g=16), func=AF.Copy),
                         [P[4 + hb], "F6"], ["F6"])
                yv = yscr[:, 0:W].rearrange("(g p) (s c) -> p g s c", p=16, s=8)
                for s_ in range(8):
                    S.dma("sp", yv[:, :, s_, :], ytv[s_ * 16:(s_ + 1) * 16, :, :], ["F6"], ["yscr"])
                Yp = Fs[7]
                S.dma("sp", Yp[:, :, 0:W], yscr.rearrange("(k p) t -> p k t", p=128)[:, :, 0:W], ["yscr"], ["F7"])
                S.end(); S.begin(n3 + "a")
                for kk in range(4):
                    S.op("dve", lambda e, kk=kk: e.scalar_tensor_tensor(out=Yp[:, kk, 0:W], in0=Usb[:, kk, 0:W], scalar=dsk[:, kk:kk + 1], in1=Yp[:, kk, 0:W],
                                                                         op0=ALU.mult, op1=ALU.add), [uk, "F7", "sm"], ["F7"])
                Y2 = Fs[8]
                S.op("act", lambda e: e.activation(out=Y2[:, :, 0:W], in_=Yp[:, :, 0:W], func=AF.Gelu_apprx_tanh), ["F7"], ["F8"])
                S.op("pool", lambda e: e.tensor_copy(out=y2b[:, :, 0:W], in_=Y2[:, :, 0:W]), ["F8"], ["y2b"])
                S.op("pool", lambda e: e.tensor_scalar(out=Yp[:, :, 0:W], in0=Y2[:, :, 0:W], scalar1=0.5, scalar2=None, op0=ALU.mult), ["F8", "F7"], ["F7"])
                TH = Fs[6]
                S.end(); S.begin(n3 + "b")
                for jo in range(4):
                    pg = ps[6][:, (jo % 2) * 256:(jo % 2) * 256 + W]
                    for kk in range(4):
                        S.op("pe", lambda e, jo=jo, kk=kk, pg=pg: e.matmul(pg, w_glu_sb[:, kk, jo * 128:(jo + 1) * 128], y2b[:, kk, 0:W], start=(kk == 0), stop=(kk == 3)),
                             ["y2b", "w_glu_sb"], [P[6]])
                    S.op("act", lambda e, jo=jo, pg=pg: e.activation(out=TH[:, jo, 0:W], in_=pg, func=AF.Tanh, scale=0.5, bias=hbg[:, jo:jo + 1]), [P[6], "cst"], ["F6"])
                S.op("dve", lambda e: e.scalar_tensor_tensor(out=mix[:, 4:8, 0:W], in0=TH[:, :, 0:W], scalar=1.0, in1=Yp[:, :, 0:W], op0=ALU.add, op1=ALU.mult),
                     ["F6", "F7"], [mkb])

            def xreload(ti):
                t0, W, kind = tiles[ti]
                S.dma("pool", HMs[:, :, 0:W], xT.rearrange("(k p) t -> p k t", p=128)[:, :, t0:t0 + W], [], ["HMs"])

            def jstream(ti, staged=True):
                t0, W, kind = tiles[ti]
                mix, mka, mkb = MIX[ti % 2], f"mixa{ti % 2}", f"mixb{ti % 2}"
                hoff = t0 - HALF
                S.begin("Jc1")
                for dc in range(8):
                    if dc == 4:
                        S.end(); S.begin("Jc2")
                    pj = ps[7][:, (dc % 2) * 256:(dc % 2) * 256 + W]
                    for k in range(8):
                        rhs = mix[:, k, 0:W]
                        if k >= 4:
                            rhs = rhs.rearrange("p (s c) -> p c s", s=8)
                        S.op("pe", lambda e, dc=dc, k=k, pj=pj, rhs=rhs: e.matmul(pj, w_out_sb[:, k, dc * 128:(dc + 1) * 128], rhs, start=(k == 0), stop=(k == 7)),
                             [mka, mkb, "w_out_sb"], [P[7]])
                    S.op("dve", lambda e, dc=dc, pj=pj: e.tensor_tensor(out=HMs[:, dc, 0:W], in0=pj, in1=HMs[:, dc, 0:W], op=ALU.add), [P[7], "HMs"], ["HMs"])
                S.dma("pool", hmid.rearrange("(k p) t -> p k t", p=128)[:, :, hoff:hoff + W], HMs[:, :, 0:W], ["HMs"], ["hmid"])

            nt1 = len(tiles)
            pre(0)
            xload(1)
            S.begin("S1xa"); sstream(0, "S2_0", "S3_0"); S.end()
            S.merge(["S1xa", "S1xb"], [(0.0, 0.5), (0.5, 1.0)])
            for ti in range(nt1 + 1):
                if ti < nt1:
                    S.begin("H"); hstream(ti); S.end()
                    if ti + 1 < nt1:
                        S.begin("S1xa"); sstream(ti + 1, f"S2_{ti + 1}", f"S3_{ti + 1}"); S.end()
                        pre(ti + 1, staged=True, do_load=False)
                        S.end()
                    if ti + 2 < nt1:
                        S.begin("Ja"); xload(ti + 2); S.end()
                    if tiles[ti][2] != "pre":
                        S.begin("Jc0"); xreload(ti); S.end()
                hasj = ti >= 1 and tiles[ti - 1][2] != "pre"
                if hasj:
                    jstream(ti - 1); S.end()
                n2, n3 = f"S2_{ti}", f"S3_{ti}"
                names = ["H", n2 + "a", n2 + "b", n2 + "c", n3 + "a", n3 + "b", "S1xa", "S1xb", "Ja", "Jb", "Jc0", "Jc1", "Jc2"]
                if ti < nt1 and tiles[ti][2] == "pre":
                    spans = [(0.0, 1.0), (0.0, 0.02), (0.3, 0.62), (0.85, 0.85), (0.85, 0.85), (0.85, 0.85), (0.22, 0.3), (0.85, 1.0),
                             (0.6, 0.61), (0.0, 0.2), (0.7, 0.71), (0.2, 0.22), (0.6, 0.66)]
                else:
                    spans = [(0.0, 1.0), (0.0, 0.02), (0.2, 0.42), (0.47, 0.58), (0.68, 0.74), (0.78, 0.88), (0.43, 0.47), (0.88, 1.0),
                             (0.6, 0.61), (0.0, 0.1), (0.7, 0.71), (0.12, 0.18), (0.6, 0.66)]
                S.merge(names, spans)
            assert not any(S.streams.values()), [k for k, v in S.streams.items() if v]
            S.barrier()
        pw.close()
        p2 = ExitStack()
        with p2:
            def vsb(name, shape, dt=F32):
                return p2.enter_context(nc.sbuf_tensor(name, list(shape), dt))
            WF = 512
            w_up_sb = vsb("w_up_sb", [128, 8, 2 * DFF], BF16)
            wsrc = w_up.rearrange("(k p) n -> p k n", p=128)
            for cbk in (0, 5, 6, 1, 7, 2, 8, 3, 9, 4, 10):
                S.dma("pool", w_up_sb[:, :, cbk * 512:(cbk + 1) * 512], wsrc[:, :, cbk * 512:(cbk + 1) * 512], [], [f"w_up{cbk}"])
            w_down_sb = vsb("w_down_sb", [128, 22, D], BF16)
            S.dma("pool", w_down_sb[:], w_down.rearrange("(j p) n -> p j n", p=128), [], ["w_down_sb"])
            hm = [vsb(f"hm{i}", [128, 8, WF]) for i in range(2)]
            hn = vsb("hn", [128, 8, WF], BF16)
            hsq = hn
            ag = [vsb(f"ag{i}", [128, WF]) for i in range(2)]
            av = [vsb(f"av{i}", [128, WF]) for i in range(2)]
            actb = vsb("actb", [128, 22, WF], BF16)
            Hh = vsb("Hh", [128, NFC, 2])
            corr = vsb("corr", [128, NFC, 2])
            ctmp = vsb("ctmp", [128, NFC])
            S.op("pool", lambda e: e.memset(Hh[:], 0.0), [], ["Hh"])
            AK = lambda j: "actb_lo" if j < 14 else "actb_hi"

            def ffn_norm(src, srck, sq, sqk, pk, W):
                S.op("act", lambda e: e.activation(out=sq[:, :, 0:W], in_=src[:, :, 0:W], func=AF.Square), [srck], [sqk])
                for k in range(8):
                    S.op("pe", lambda e, k=k: e.matmul(ps[pk][:, 0:W], ones_b[:], sq[:, k, 0:W], start=(k == 0), stop=(k == 7)), [sqk, "ones_b"], [P[pk]])
                S.op("act", lambda e: e.activation(out=ps[pk][:, 0:W], in_=ps[pk][:, 0:W], func=AF.Ln, scale=1.0 / D, bias=epsc), [P[pk], "cst"], [P[pk]])
                S.op("act", lambda e: e.activation(out=ps[pk][:, 0:W], in_=ps[pk][:, 0:W], func=AF.Exp, scale=-0.5), [P[pk]], [P[pk]])

            ftiles = [(0, WARM, "warm")] + [(WARM + i * WF, WF, "main") for i in range(HALF // WF)]

            def prep(ti, part="AB", early=False):
                t0, W, kind = ftiles[ti]
                hb, hk = hm[ti % 2], f"hm{ti % 2}"
                if "A" in part:
                    S.dma("sp", hb[:, :, 0:W], hmid.rearrange("(k p) t -> p k t", p=128)[:, :, t0:t0 + W], [], [hk])
                    if early:
                        ffn_norm(hb, hk, actb[:, 14:22, :], "actb_hi", 7, W)
                    else:
                        ffn_norm(hb, hk, hsq, "hn", 7, W)
                if "B" not in part:
                    return
                for k in range(8):
                    S.op("dve", lambda e, k=k: e.scalar_tensor_tensor(out=hn[:, k, 0:W], in0=hb[:, k, 0:W], scalar=g_ffn[:, k:k + 1], in1=ps[7][:, 0:W],
                                                                       op0=ALU.mult, op1=ALU.mult), [hk, P[7], "sm"], ["hn"])

            def up(ti):
                t0, W, kind = ftiles[ti]
                main = kind == "main"
                if main:
                    S.op("dve", lambda e: e.tensor_tensor(out=ctmp[:], in0=Hh[:, :, 1], in1=cw[:, 1, :], op=ALU.mult), ["Hh", "sm"], ["ctmp"])
                    S.op("dve", lambda e: e.tensor_tensor(out=corr[:, :, 0], in0=Hh[:, :, 0], in1=cw[:, 0, :], op=ALU.mult), ["Hh", "sm"], ["corr"])
                    S.op("dve", lambda e: e.tensor_tensor(out=corr[:, :, 0], in0=corr[:, :, 0], in1=ctmp[:], op=ALU.add), ["corr", "ctmp"], ["corr"])
                    S.op("dve", lambda e: e.tensor_tensor(out=corr[:, :, 1], in0=Hh[:, :, 1], in1=cw[:, 0, :], op=ALU.mult), ["Hh", "sm", "corr"], ["corr"])
                for j in range(22):
                    sl = j % 2
                    for (c, abuf, nm, pk) in ((j, ag[sl], f"ag{sl}", (j % 3) * 2), (22 + j, av[sl], f"av{sl}", (j % 3) * 2 + 1)):
                        pt = ps[pk]
                        for k in range(8):
                            S.op("pe", lambda e, k=k, c=c, pt=pt: e.matmul(pt[:, 0:W], w_up_sb[:, k, c * 128:(c + 1) * 128], hn[:, k, 0:W], start=(k == 0), stop=(k == 7)),
                                 ["hn", f"w_up{c // 4}"], [P[pk]])
                        if main:
                            S.op("act", lambda e, c=c, pt=pt, abuf=abuf: e.activation(out=abuf[:, 0:W], in_=pt[:, 0:W], func=AF.Identity, scale=cw[:, 2, c:c + 1], bias=cb[:, c:c + 1]),
                                 [P[pk], "sm"], [nm])
                            S.op("dve", lambda e, c=c, pt=pt, abuf=abuf: e.scalar_tensor_tensor(out=abuf[:, 1:W], in0=pt[:, 0:W - 1], scalar=cw[:, 1, c:c + 1], in1=abuf[:, 1:W],
                                                                                                 op0=ALU.mult, op1=ALU.add), [P[pk], "sm", nm], [nm])
                            S.op("dve", lambda e, c=c, pt=pt, abuf=abuf: e.scalar_tensor_tensor(out=abuf[:, 2:W], in0=pt[:, 0:W - 2], scalar=cw[:, 0, c:c + 1], in1=abuf[:, 2:W],
                                                                                                 op0=ALU.mult, op1=ALU.add), [P[pk], "sm", nm], [nm])
                            S.op("pool", lambda e, c=c, abuf=abuf: e.tensor_tensor(out=abuf[:, 0:2], in0=abuf[:, 0:2], in1=corr[:, c, :], op=ALU.add), ["corr", nm], [nm])
                        S.op("act", lambda e, c=c, pt=pt: e.activation(out=Hh[:, c, :], in_=pt[:, W - 2:W], func=AF.Copy), [P[pk], "Hh"], ["Hh"])
                    if main:
                        S.op("act", lambda e, sl=sl: e.activation(out=ag[sl][:, 0:W], in_=ag[sl][:, 0:W], func=AF.Silu), [f"ag{sl}"], [f"ag{sl}"])
                        S.op("pool", lambda e, sl=sl, j=j: e.tensor_tensor(out=actb[:, j, 0:W], in0=ag[sl][:, 0:W], in1=av[sl][:, 0:W], op=ALU.mult), [f"ag{sl}", f"av{sl}"], [AK(j)])

            def down(ti):
                t0, W, kind = ftiles[ti]
                hb, hk = hm[ti % 2], f"hm{ti % 2}"
                for dc in range(8):
                    pk = 4 + dc % 2
                    for j in range(22):
                        S.op("pe", lambda e, dc=dc, j=j, pk=pk: e.matmul(ps[pk][:, 0:W], w_down_sb[:, j, dc * 128:(dc + 1) * 128], actb[:, j, 0:W], start=(j == 0), stop=(j == 21)),
                             [AK(j), "w_down_sb"], [P[pk]])
                    S.op("dve", lambda e, dc=dc, pk=pk: e.tensor_tensor(out=hb[:, dc, 0:W], in0=ps[pk][:, 0:W], in1=hb[:, dc, 0:W], op=ALU.add), [P[pk], hk], [hk])

            def fin(ti):
                t0, W, kind = ftiles[ti]
                hb, hk = hm[ti % 2], f"hm{ti % 2}"
                fsq = actb[:, 14:22, :]
                ffn_norm(hb, hk, fsq, "actb_hi", 6, W)
                for k in range(8):
                    S.op("dve", lambda e, k=k: e.scalar_tensor_tensor(out=hb[:, k, 0:W], in0=hb[:, k, 0:W], scalar=g_fin[:, k:k + 1], in1=ps[6][:, 0:W],
                                                                       op0=ALU.mult, op1=ALU.mult), [hk, P[6], "sm"], [hk])
                S.dma("sp", outT.rearrange("(k p) t -> p k t", p=128)[:, :, t0 - WARM:t0 - WARM + W], hb[:, :, 0:W], [hk], ["outT"])

            prep(0)
            up(0)
            prep(1)
            nt = len(ftiles)
            for ti in range(1, nt):
                S.begin("up"); up(ti); S.end()
                if ti + 1 < nt:
                    S.begin("pa"); prep(ti + 1, "A", early=True); S.end()
                S.merge(["up", "fin", "pa"], [(0.0, 1.0), (0.0, 0.2), (0.25, 0.45)])
                if ti + 1 < nt:
                    prep(ti + 1, "B")
                down(ti)
                S.begin("fin"); fin(ti); S.end()
            S.merge(["fin"])
            S.barrier()
            S.finalize()
            block = es.enter_context(nc.Block())
            block.sync(lambda e: S.replay("sp", e))
            block.tensor(lambda e: S.replay("pe", e))
            block.scalar(lambda e: S.replay("act", e))
            block.vector(lambda e: S.replay("dve", e))
            block.gpsimd(lambda e: S.replay("pool", e))
    return nc


def _pk(v, k):
    return np.ascontiguousarray(np.asarray(v, np.float32).reshape(k, 128).T)


def kernel(x, in_norm_g, w_in, hg_lb, hg_norm_g, s5_a_re, s5_a_im, s5_log_dt, s5_b_re, s5_b_im,
           s5_c_re, s5_c_im, s5_d, s5_w_glu, s5_b_glu, w_out, ffn_norm_g, w_up, conv_w, conv_b,
           w_down, final_norm_g):
    f = lambda a: np.asarray(a, np.float32)
    x = f(x)
    smalls = np.concatenate([
        _pk(f(in_norm_g)[0], 8), _pk(f(ffn_norm_g)[0], 8), _pk(f(final_norm_g), 8),
        np.concatenate([_pk(f(hg_lb)[0], 4), _pk(f(hg_lb)[1], 4)], axis=1),
        _pk(f(hg_norm_g)[0], 4), _pk(f(s5_d)[0], 4), _pk(f(s5_b_glu)[0], 4),
        np.concatenate([_pk(f(conv_w)[0, j], NFC) for j in range(3)], axis=1),
        _pk(f(conv_b)[0], NFC)], axis=1)

    def pairpack(a):
        a = f(a)
        r = a.reshape((16, 2, 64) + a.shape[2:])
        r = np.moveaxis(r, 0, 2)
        return np.ascontiguousarray(r.reshape((128, 16) + a.shape[2:]))
    are = pairpack(f(s5_a_re)[0]); aim = pairpack(f(s5_a_im)[0])
    ldt = pairpack(np.repeat(f(s5_log_dt)[0][:, None], 64, axis=1))
    s5s = np.concatenate([are, aim, ldt], axis=1)
    s5B = np.concatenate([pairpack(f(s5_b_re)[0]).reshape(128, -1), pairpack(f(s5_b_im)[0]).reshape(128, -1)], axis=1)
    ctre = np.transpose(f(s5_c_re)[0], (0, 2, 1)); ctim = np.transpose(f(s5_c_im)[0], (0, 2, 1))
    s5C = np.concatenate([pairpack(ctre).reshape(128, -1), pairpack(ctim).reshape(128, -1)], axis=1)
    common = {"w_in": f(w_in)[0], "w_out": f(w_out)[0], "w_glu": f(s5_w_glu)[0], "w_up": f(w_up)[0], "w_down": f(w_down)[0],
              "smalls": np.ascontiguousarray(smalls), "s5s": np.ascontiguousarray(s5s), "s5B": np.ascontiguousarray(s5B), "s5C": np.ascontiguousarray(s5C)}
    in_maps = []
    for c in range(8):
        b, h = c // 2, c % 2
        win = np.zeros((WIN, D), np.float32)
        start = h * HALF - (HALF + WARM)
        lo = max(start, 0)
        win[lo - start:] = x[b, lo:h * HALF + HALF]
        m = dict(common)
        m["xT"] = np.ascontiguousarray(win.T)
        in_maps.append(m)
    nc = build_program()
    res = run_bass_kernel_spmd(nc, in_maps, core_ids=list(range(8)))
    out = np.empty((4, SEQ, D), np.float32)
    for c in range(8):
        b, h = c // 2, c % 2
        out[b, h * HALF:(h + 1) * HALF] = res.results[c]["outT"].T
    return out
```

```python
import math
from contextlib import ExitStack
import numpy as np
import concourse.bass as bass
import concourse.mybir as mybir
from concourse.bass_utils import run_bass_kernel_spmd

F32 = mybir.dt.float32
BF16 = mybir.dt.bfloat16
AF = mybir.ActivationFunctionType
ALU = mybir.AluOpType

D = 1024
SEQ = 4096
HALF = 2048
DFF = 2816
NFC = 44
WT = 256
WARM = 64
NPRE = HALF // WT
NMAIN = HALF // WT
WIN = HALF + WARM + HALF
EPS = 1e-6


class Sched:
    def __init__(self, nc, es):
        self.nc, self.es = nc, es
        self.engs = {"pe": nc.tensor, "act": nc.scalar, "dve": nc.vector, "pool": nc.gpsimd, "sp": nc.sync}
        self.q = {e: [] for e in self.engs}
        self.sems, self.count = {}, {}
        self.waited = {e: {} for e in self.engs}
        self.lastw, self.readers = {}, {}
        self.cur, self.streams = None, {}
        for e in self.engs:
            self._newsem("E_" + e)

    def _newsem(self, k):
        self.sems[k] = self.es.enter_context(self.nc.semaphore(k))
        self.count[k] = 0

    def _deps(self, eng, reads, writes):
        deps = {}

        def need(sv):
            deps[sv[0]] = max(deps.get(sv[0], 0), sv[1])

        for k in reads:
            if k in self.lastw:
                need(self.lastw[k])
        for k in writes:
            if k in self.lastw:
                need(self.lastw[k])
            for r in self.readers.get(k, ()):
                need(r)
        out = []
        for sk, v in deps.items():
            if eng == "pe" and sk == "E_pe":
                continue
            if self.waited[eng].get(sk, 0) >= v:
                continue
            self.waited[eng][sk] = v
            out.append((sk, v))
        return out

    def _reg(self, sk, v, reads, writes):
        for k in reads:
            self.readers.setdefault(k, []).append((sk, v))
        for k in writes:
            self.lastw[k] = (sk, v)
            self.readers[k] = []

    def begin(self, name):
        self.cur = name
        self.streams.setdefault(name, [])

    def end(self):
        self.cur = None

    def merge(self, names, spans=None):
        items = []
        for j, n in enumerate(names):
            l = self.streams.pop(n, [])
            lo, hi = (0.0, 1.0) if spans is None else spans[j]
            for k, it in enumerate(l):
                items.append((lo + (hi - lo) * (k + 0.5) / len(l), j, k, it))
        items.sort(key=lambda t: t[:3])
        for _, _, _, (kind, a) in items:
            (self.op if kind == "op" else self.dma)(*a)

    def op(self, eng, fn, reads=(), writes=()):
        if self.cur is not None:
            self.streams[self.cur].append(("op", (eng, fn, reads, writes)))
            return
        waits = self._deps(eng, reads, writes)
        sk = "E_" + eng
        self.count[sk] += 1
        self.q[eng].append((waits, fn, sk, 1))
        self._reg(sk, self.count[sk], reads, writes)

    def dma(self, eng, out, in_, reads, writes):
        if self.cur is not None:
            self.streams[self.cur].append(("dma", (eng, out, in_, reads, writes)))
            return
        sk = "D_" + writes[0]
        if sk not in self.sems:
            self._newsem(sk)
        waits = self._deps(eng, reads, writes)
        self.count[sk] += 16
        self.q[eng].append((waits, lambda e, o=out, i=in_: e.dma_start(out=o, in_=i), sk, 16))
        self._reg(sk, self.count[sk], reads, writes)

    def barrier(self):
        for e in self.engs:
            waits = []
            for sk, c in self.count.items():
                if c > 0 and self.waited[e].get(sk, 0) < c:
                    self.waited[e][sk] = c
                    waits.append((sk, c))
            self.q[e].append((waits, None, None, 0))
        self.lastw, self.readers = {}, {}

    def finalize(self):
        need = {}
        for e in self.engs:
            for waits, fn, sk, inc in self.q[e]:
                for wk, wv in waits:
                    if wk.startswith("E_"):
                        need.setdefault(wk, set()).add(wv)
        self.rank = {}
        for wk, vs in need.items():
            for r, v in enumerate(sorted(vs)):
                self.rank[(wk, v)] = r + 1

    def replay(self, name, eng):
        n = 0
        for waits, fn, sk, inc in self.q[name]:
            for wk, wv in waits:
                eng.wait_ge(self.sems[wk], self.rank[(wk, wv)] if wk.startswith("E_") else wv)
            if fn is not None:
                ins = fn(eng)
                if inc == 16:
                    ins.then_inc(self.sems[sk], 16)
                else:
                    n += 1
                    if (sk, n) in self.rank:
                        ins.then_inc(self.sems[sk], 1)


def bc(ap, shape):
    return ap.broadcast_to(list(shape))


def build_program():
    nc = bass.Bass("TRN2", target_bir_lowering=False)
    es = ExitStack()
    with es:
        S = Sched(nc, es)

        def dram_in(name, shape):
            return nc.dram_tensor(name, list(shape), F32, kind="ExternalInput").ap()

        xT = dram_in("xT", [D, WIN])
        w_in = dram_in("w_in", [D, 2560])
        w_out = dram_in("w_out", [D, D])
        w_glu = dram_in("w_glu", [512, 512])
        w_up = dram_in("w_up", [D, 2 * DFF])
        w_down = dram_in("w_down", [DFF, D])
        smalls = dram_in("smalls", [128, 8 * 3 + 8 + 4 + 4 + 4 + 3 * NFC + NFC])
        s5s = dram_in("s5s", [128, 16 * 3])
        s5B = dram_in("s5B", [128, 2 * 16 * 16])
        s5C = dram_in("s5C", [128, 2 * 16 * 16])
        outT = nc.dram_tensor("outT", [D, HALF], F32, kind="ExternalOutput").ap()
        uscr = nc.dram_tensor("uscr", [512, WT], F32, kind="Internal").ap()
        yscr = nc.dram_tensor("yscr", [512, WT], F32, kind="Internal").ap()
        hmid = nc.dram_tensor("hmid", [D, WARM + HALF], F32, kind="Internal").ap()

        def sb(name, shape, dt=F32):
            return es.enter_context(nc.sbuf_tensor(name, list(shape), dt))

        ps = [es.enter_context(nc.psum_tensor(f"ps{i}", [128, 512], F32)) for i in range(8)]
        P = [f"ps{i}" for i in range(8)]

        sm = sb("sm", [128, 8 * 3 + 8 + 4 + 4 + 4 + 3 * NFC + NFC])
        o = 0
        g_in = sm[:, 0:8]; g_ffn = sm[:, 8:16]; g_fin = sm[:, 16:24]; o = 24
        lbraw = sm[:, o:o + 8]; o += 8
        ngv = sm[:, o:o + 4]; o += 4
        dsk = sm[:, o:o + 4]; o += 4
        bgv = sm[:, o:o + 4]; o += 4
        cw = sm[:, o:o + 3 * NFC].rearrange("p (j c) -> p j c", j=3); o += 3 * NFC
        cb = sm[:, o:o + NFC]
        cst = sb("cst", [128, 64])
        lb = cst[:, 0:4]; oml = cst[:, 4:8]; lnoml = cst[:, 8:12]; hbg = cst[:, 12:16]
        epsc = cst[:, 16:17]; halfpi = cst[:, 17:18]; tmpc = cst[:, 20:28]
        ones_b = sb("ones_b", [128, 128], BF16)
        pw = ExitStack()

        def pwsb(name, shape, dt=F32):
            return pw.enter_context(nc.sbuf_tensor(name, list(shape), dt))
        ident_f = pwsb("ident_f", [128, 128], F32)
        ident_b = pwsb("ident_b", [128, 128], BF16)
        mask2 = pwsb("mask2", [128, 64], F32)
        mreset = pwsb("mreset", [128, WT], F32)
        w_in_sb = pwsb("w_in_sb", [128, 8, 2560], BF16)
        w_out_sb = pwsb("w_out_sb", [128, 8, D], BF16)
        w_glu_sb = pwsb("w_glu_sb", [128, 4, 512], BF16)
        C8 = WT // 8
        Kt = pwsb("Kt", [128, 32, 128], BF16)
        BAre = pwsb("BAre", [128, 32, 64], BF16)
        BAim = pwsb("BAim", [128, 32, 64], BF16)
        CAre = pwsb("CAre", [128, 16, 128], BF16)
        CAimn = pwsb("CAimn", [128, 16, 128], BF16)
        rcos = pwsb("rcos", [128, 16, C8])
        rsin = pwsb("rsin", [128, 16, C8])
        R8T = pwsb("R8T", [128, 16, C8])
        r8 = pwsb("r8", [128, 16])
        xc = pwsb("xc", [128, 2, 16])
        Sst = [pwsb(f"Sst{i}", [128, 512]) for i in range(2)]
        NCH = WT // 64
        S_bf = pwsb("S_bf", [128, NCH + 1, 512], BF16)

        S.dma("sp", sm[:], smalls[:, :], [], ["sm"])
        s5sm = pwsb("s5sm", [128, 48])
        S.dma("sp", s5sm[:], s5s[:, :], [], ["s5sm"])
        S.dma("pool", w_in_sb[:], w_in.rearrange("(k p) n -> p k n", p=128), [], ["w_in_sb"])
        S.dma("pool", w_out_sb[:], w_out.rearrange("(k p) n -> p k n", p=128), [], ["w_out_sb"])
        S.dma("pool", w_glu_sb[:], w_glu.rearrange("(k p) n -> p k n", p=128), [], ["w_glu_sb"])

        S.op("pool", lambda e: e.memset(cst[:], 0.0), [], ["cst"])
        S.op("pool", lambda e: e.memset(epsc, EPS), [], ["cst"])
        S.op("pool", lambda e: e.memset(halfpi, math.pi / 2), [], ["cst"])
        S.op("pool", lambda e: e.memset(ones_b[:], 1.0), [], ["ones_b"])
        S.op("pool", lambda e: e.memset(ident_f[:], 0.0), [], ["ident_f"])
        S.op("pool", lambda e: e.affine_select(out=ident_f[:], in_=ident_f[:], pattern=[[-1, 128]], compare_op=ALU.not_equal,
                                               fill=1.0, base=0, channel_multiplier=1), ["ident_f"], ["ident_f"])
        S.op("pool", lambda e: e.tensor_copy(out=ident_b[:], in_=ident_f[:]), ["ident_f"], ["ident_b"])
        S.op("pool", lambda e: e.memset(mask2[:], 1.0), [], ["mask2"])
        S.op("pool", lambda e: e.affine_select(out=mask2[0:64, :], in_=mask2[0:64, :], pattern=[[1, 64]], compare_op=ALU.is_ge,
                                               fill=0.0, base=0, channel_multiplier=-1), ["mask2"], ["mask2"])
        S.dma("sp", mask2[64:128, :], mask2[0:64, :], ["mask2"], ["mask2"])
        S.op("pool", lambda e: e.memset(mreset[:], 1.0), [], ["mreset"])
        S.op("pool", lambda e: e.memset(mreset[:].rearrange("p (n c) -> p n c", c=64)[:, :, 0:1], 0.0), ["mreset"], ["mreset"])
        S.op("pool", lambda e: e.memset(xc[:], 0.0), [], ["xc"])
        S.op("pool", lambda e: e.memset(Sst[0][:], 0.0), [], ["Sst0"])
        S.op("pool", lambda e: e.memset(S_bf[:], 0.0), [], ["S_bf"])
        S.op("dve", lambda e: e.tensor_tensor(out=tmpc[:, 0:4], in0=lbraw[:, 4:8], in1=lbraw[:, 0:4], op=ALU.subtract), ["sm", "cst"], ["cst"])
        S.op("act", lambda e: e.activation(out=tmpc[:, 4:8], in_=tmpc[:, 0:4], func=AF.Exp), ["cst"], ["cst"])
        S.op("dve", lambda e: e.tensor_scalar(out=tmpc[:, 0:4], in0=tmpc[:, 4:8], scalar1=1.0, scalar2=None, op0=ALU.add), ["cst"], ["cst"])
        S.op("dve", lambda e: e.reciprocal(out=lb, in_=tmpc[:, 0:4]), ["cst"], ["cst"])
        S.op("dve", lambda e: e.tensor_tensor(out=oml, in0=tmpc[:, 4:8], in1=lb, op=ALU.mult), ["cst"], ["cst"])
        S.op("act", lambda e: e.activation(out=lnoml, in_=oml, func=AF.Ln), ["cst"], ["cst"])
        S.op("dve", lambda e: e.tensor_scalar(out=hbg, in0=bgv, scalar1=0.5, scalar2=None, op0=ALU.mult), ["sm", "cst"], ["cst"])

        pes = ExitStack()
        with pes:
            def tsb(name, shape, dt=F32):
                return pes.enter_context(nc.sbuf_tensor(name, list(shape), dt))
            Bsb = tsb("Bsb", [128, 2, 16, 16]); Csb = tsb("Csb", [128, 2, 16, 16])
            S.dma("sp", Bsb[:].rearrange("p a q c -> p (a q c)"), s5B[:, :], [], ["Bsb"])
            S.dma("sp", Csb[:].rearrange("p a q c -> p (a q c)"), s5C[:, :], [], ["Csb"])
            are = s5sm[:, 0:16]; aim = s5sm[:, 16:32]; ldt = s5sm[:, 32:48]
            t16 = tsb("t16", [128, 12, 16])
            tv = lambda i: t16[:, i, :]
            K16 = ["t16"]

            def dv(fn):
                S.op("dve", fn, K16 + ["s5sm"], K16)

            def ac(fn):
                S.op("act", fn, K16 + ["s5sm", "cst"], K16)
            DT, DAR, DAI, MAG, CS, SN, T0, T1, T2, ZR, ZI, DEN = range(12)
            ac(lambda e: e.activation(out=tv(DT), in_=ldt, func=AF.Exp))
            dv(lambda e: e.tensor_tensor(out=tv(DAR), in0=tv(DT), in1=are, op=ALU.mult))
            dv(lambda e: e.tensor_tensor(out=tv(DAI), in0=tv(DT), in1=aim, op=ALU.mult))
            ac(lambda e: e.activation(out=tv(MAG), in_=tv(DAR), func=AF.Exp))
            ac(lambda e: e.activation(out=tv(SN), in_=tv(DAI), func=AF.Sin, scale=1.0 / 64))
            ac(lambda e: e.activation(out=tv(CS), in_=tv(DAI), func=AF.Sin, scale=1.0 / 64, bias=halfpi))
            for _ in range(6):
                dv(lambda e: e.tensor_tensor(out=tv(T0), in0=tv(CS), in1=tv(CS), op=ALU.mult))
                dv(lambda e: e.tensor_tensor(out=tv(T1), in0=tv(SN), in1=tv(SN), op=ALU.mult))
                dv(lambda e: e.scalar_tensor_tensor(out=tv(SN), in0=tv(SN), scalar=2.0, in1=tv(CS), op0=ALU.mult, op1=ALU.mult))
                dv(lambda e: e.tensor_tensor(out=tv(CS), in0=tv(T0), in1=tv(T1), op=ALU.subtract))
            TPr = tsb("TPr", [128, 16, 17]); TPi = tsb("TPi", [128, 16, 17])
            KT = ["TP", "t16"]

            def dt_(fn):
                S.op("dve", fn, KT, KT)
            dt_(lambda e: e.memset(TPr[:, :, 8:9], 1.0))
            dt_(lambda e: e.memset(TPi[:, :, 8:9], 0.0))
            dt_(lambda e: e.tensor_tensor(out=TPr[:, :, 9], in0=tv(MAG), in1=tv(CS), op=ALU.mult))
            dt_(lambda e: e.tensor_tensor(out=TPi[:, :, 9], in0=tv(MAG), in1=tv(SN), op=ALU.mult))
            dt_(lambda e: e.tensor_tensor(out=tv(T0), in0=tv(MAG), in1=tv(MAG), op=ALU.mult))
            dt_(lambda e: e.reciprocal(out=tv(T1), in_=tv(T0)))
            dt_(lambda e: e.tensor_tensor(out=TPr[:, :, 7], in0=TPr[:, :, 9], in1=tv(T1), op=ALU.mult))
            dt_(lambda e: e.scalar_tensor_tensor(out=TPi[:, :, 7], in0=TPi[:, :, 9], scalar=-1.0, in1=tv(T1), op0=ALU.mult, op1=ALU.mult))
            cmt = tsb("cmt", [128, 4, 16, 16])

            def cmul(key, outr, outi, ar, ai, br, bi, shape, eng="dve"):
                n = 1
                for s_ in shape:
                    n *= s_
                tt = [cmt[:, i].rearrange("p a b -> p (a b)")[:, 0:n] for i in range(4)]
                if len(shape) == 2:
                    tt = [t.rearrange("p (a b) -> p a b", a=shape[0]) for t in tt]
                k = key + ["cmt"]
                S.op(eng, lambda e: e.tensor_tensor(out=tt[0], in0=ar, in1=br, op=ALU.mult), k, k)
                S.op(eng, lambda e: e.tensor_tensor(out=tt[1], in0=ai, in1=bi, op=ALU.mult), k, k)
                S.op(eng, lambda e: e.tensor_tensor(out=tt[2], in0=ar, in1=bi, op=ALU.mult), k, k)
                S.op(eng, lambda e: e.tensor_tensor(out=tt[3], in0=ai, in1=br, op=ALU.mult), k, k)
                S.op(eng, lambda e: e.tensor_tensor(out=outr, in0=tt[0], in1=tt[1], op=ALU.subtract), k, k)
                S.op(eng, lambda e: e.tensor_tensor(out=outi, in0=tt[2], in1=tt[3], op=ALU.add), k, k)
            for (lo, n, step) in ((9, 1, 1), (9, 2, 2), (9, 4, 4)):
                cmul(KT, TPr[:, :, lo + step:lo + step + n], TPi[:, :, lo + step:lo + step + n],
                     TPr[:, :, lo:lo + n], TPi[:, :, lo:lo + n],
                     bc(TPr[:, :, 8 + step:9 + step], [128, 16, n]), bc(TPi[:, :, 8 + step:9 + step], [128, 16, n]), [16, n])
            for (lo, n, step) in ((7, 1, 1), (6, 2, 2), (4, 4, 4)):
                cmul(KT, TPr[:, :, lo - step:lo - step + n], TPi[:, :, lo - step:lo - step + n],
                     TPr[:, :, lo:lo + n], TPi[:, :, lo:lo + n],
                     bc(TPr[:, :, 8 - step:9 - step], [128, 16, n]), bc(TPi[:, :, 8 - step:9 - step], [128, 16, n]), [16, n])
            dt_(lambda e: e.tensor_scalar(out=tv(T0), in0=TPr[:, :, 9], scalar1=-1.0, scalar2=None, op0=ALU.add))
            S.op("dve", lambda e: e.tensor_tensor(out=tv(T1), in0=are, in1=are, op=ALU.mult), KT + ["s5sm"], KT)
            S.op("dve", lambda e: e.tensor_tensor(out=tv(T2), in0=aim, in1=aim, op=ALU.mult), KT + ["s5sm"], KT)
            dt_(lambda e: e.tensor_tensor(out=tv(DEN), in0=tv(T1), in1=tv(T2), op=ALU.add))
            dt_(lambda e: e.reciprocal(out=tv(DEN), in_=tv(DEN)))
            S.op("dve", lambda e: e.tensor_scalar(out=tv(T2), in0=aim, scalar1=-1.0, scalar2=None, op0=ALU.mult), KT + ["s5sm"], KT)
            S.op("dve", lambda e: e.tensor_copy(out=tv(T1), in_=are), KT + ["s5sm"], KT)
            cmul(KT, tv(ZR), tv(ZI), tv(T0), TPi[:, :, 9], tv(T1), tv(T2), [16])
            dt_(lambda e: e.tensor_tensor(out=tv(ZR), in0=tv(ZR), in1=tv(DEN), op=ALU.mult))
            dt_(lambda e: e.tensor_tensor(out=tv(ZI), in0=tv(ZI), in1=tv(DEN), op=ALU.mult))
            cRr = tsb("cRr", [128, 16, 8]); cRi = tsb("cRi", [128, 16, 8])
            KT2 = KT + ["cR"]
            for s_ in range(8):
                cmul(KT2, cRr[:, :, s_], cRi[:, :, s_], TPr[:, :, 15 - s_], TPi[:, :, 15 - s_], tv(ZR), tv(ZI), [16])
            Rre = tsb("Rre", [128, 16, 8, 16]); Rim = tsb("Rim", [128, 16, 8, 16])
            Qre = tsb("Qre", [128, 16, 8, 16]); Qimn = tsb("Qimn", [128, 16, 8, 16])
            big = tsb("bigt", [128, 2, 16, 8, 16])

            def cmul_big(key, outr, outi, cr, ci, br, bi, neg_im=False):
                crb = bc(cr.unsqueeze(3), [128, 16, 8, 16]); cib = bc(ci.unsqueeze(3), [128, 16, 8, 16])
                brb = bc(br.unsqueeze(2), [128, 16, 8, 16]); bib = bc(bi.unsqueeze(2), [128, 16, 8, 16])
                k = key + ["bigt"]
                S.op("dve", lambda e: e.tensor_tensor(out=big[:, 0], in0=crb, in1=brb, op=ALU.mult), k, k)
                S.op("dve", lambda e: e.tensor_tensor(out=big[:, 1], in0=cib, in1=bib, op=ALU.mult), k, k)
                S.op("dve", lambda e: e.tensor_tensor(out=outr, in0=big[:, 0], in1=big[:, 1], op=ALU.subtract), k, k)
                S.op("dve", lambda e: e.tensor_tensor(out=big[:, 0], in0=crb, in1=bib, op=ALU.mult), k, k)
                S.op("dve", lambda e: e.tensor_tensor(out=big[:, 1], in0=cib, in1=brb, op=ALU.mult), k, k)
                if neg_im:
                    S.op("dve", lambda e: e.scalar_tensor_tensor(out=outi, in0=big[:, 0], scalar=-1.0, in1=big[:, 1], op0=ALU.mult, op1=ALU.subtract), k, k)
                else:
                    S.op("dve", lambda e: e.tensor_tensor(out=outi, in0=big[:, 0], in1=big[:, 1], op=ALU.add), k, k)
            KB = KT2 + ["Bsb", "Csb", "RQ"]
            cmul_big(KB, Rre[:], Rim[:], cRr[:], cRi[:], Bsb[:, 0], Bsb[:, 1])
            cmul_big(KB, Qre[:], Qimn[:], TPr[:, :, 1:9], TPi[:, :, 1:9], Csb[:, 0], Csb[:, 1], neg_im=True)
            mk = tsb("mk", [128, 8, 16])
            S.op("pool", lambda e: e.memset(mk[:], 1.0), [], ["mk"])
            S.op("pool", lambda e: e.affine_select(out=mk[:], in_=mk[:], pattern=[[16, 8], [0, 16]], compare_op=ALU.is_ge, fill=0.0,
                                                   base=15, channel_multiplier=-1), ["mk"], ["mk"])
            for g in range(32):
                q_, gi = g // 2, g % 2
                rs = slice(gi * 64, gi * 64 + 64)
                pt = ps[g % 4]
                S.op("pe", lambda e, q_=q_, rs=rs, pt=pt: e.matmul(pt[:, 0:128], Rre[rs, q_].rearrange("p a b -> p (a b)"),
                                                                    Qre[rs, q_].rearrange("p a b -> p (a b)"), start=True, stop=False), KB, [P[g % 4]])
                S.op("pe", lambda e, q_=q_, rs=rs, pt=pt: e.matmul(pt[:, 0:128], Rim[rs, q_].rearrange("p a b -> p (a b)"),
                                                                    Qimn[rs, q_].rearrange("p a b -> p (a b)"), start=False, stop=True), KB, [P[g % 4]])
                S.op("dve", lambda e, g=g, pt=pt: e.tensor_tensor(out=Kt[:, g, :], in0=pt[:, 0:128], in1=mk[:].rearrange("p a b -> p (a b)"), op=ALU.mult),
                     [P[g % 4], "mk"], ["Kt"])
                for (Rt, BAt, nm) in ((Rre, BAre, "BAre"), (Rim, BAim, "BAim")):
                    pt2 = ps[4 + (g % 2) * 2 + (0 if nm == "BAre" else 1)]
                    pk = P[4 + (g % 2) * 2 + (0 if nm == "BAre" else 1)]
                    S.op("pe", lambda e, Rt=Rt, pt2=pt2, q_=q_, rs=rs: e.transpose(pt2[:, 0:64], Rt[rs, q_].rearrange("p a b -> p (a b)"), ident_f[rs, rs]),
                         KB + ["ident_f"], [pk])
                    S.op("act", lambda e, BAt=BAt, pt2=pt2, g=g: e.activation(out=BAt[:, g, :], in_=pt2[:, 0:64], func=AF.Copy), [pk], [nm])
            cmul_big(KB, Qre[:], Qimn[:], TPr[:, :, 9:17], TPi[:, :, 9:17], Csb[:, 0], Csb[:, 1], neg_im=True)
            S.op("act", lambda e: e.activation(out=CAre[:].rearrange("p q m -> p (q m)"), in_=Qre[:].rearrange("p q a b -> p (q a b)"), func=AF.Copy), KB, ["CAre"])
            S.op("act", lambda e: e.activation(out=CAimn[:].rearrange("p q m -> p (q m)"), in_=Qimn[:].rearrange("p q a b -> p (q a b)"), func=AF.Copy), KB, ["CAimn"])
            S.op("act", lambda e: e.activation(out=r8[:], in_=tv(DAR), func=AF.Exp, scale=8.0), KT, ["r8"])
            S.op("dve", lambda e: e.reciprocal(out=tv(T0), in_=r8[:]), KT + ["r8"], KT)
            S.op("dve", lambda e: e.tensor_tensor(out=rcos[:, :, 0], in0=TPr[:, :, 16], in1=tv(T0), op=ALU.mult), KT + ["r8"], ["rot"])
            S.op("dve", lambda e: e.tensor_tensor(out=rsin[:, :, 0], in0=TPi[:, :, 16], in1=tv(T0), op=ALU.mult), KT, ["rot"])
            n_ = 1
            while n_ < C8:
                cmul(["rot"], rcos[:, :, n_:2 * n_], rsin[:, :, n_:2 * n_], rcos[:, :, 0:n_], rsin[:, :, 0:n_],
                     bc(rcos[:, :, n_ - 1:n_], [128, 16, n_]), bc(rsin[:, :, n_ - 1:n_], [128, 16, n_]), [16, n_])
                n_ *= 2
            S.op("dve", lambda e: e.tensor_copy(out=R8T[:], in_=bc(r8[:].unsqueeze(2), [128, 16, C8])), ["r8"], ["R8T"])
            S.op("dve", lambda e: e.memset(R8T[:, :, 0:1], 0.0), ["R8T"], ["R8T"])
            S.barrier()
        p1 = ExitStack()
        with p1:
            def wsb(name, shape, dt=F32):
                return p1.enter_context(nc.sbuf_tensor(name, list(shape), dt))
            xt = wsb("xt", [128, 8, WT])
            xsq = wsb("xsq", [128, 8, WT], BF16)
            rstd = wsb("rstd", [128, 2, WT])
            XN = [wsb(f"xn{i}", [128, 8, WT], BF16) for i in range(2)]
            MIX = [wsb(f"mix{i}", [128, 8, WT], BF16) for i in range(2)]
            HMs = wsb("HMs", [128, 8, WT])
            Fs = [wsb(f"F{i}", [128, 4, WT]) for i in range(10)]
            ktT = wsb("ktT", [128, 4, WT], BF16)
            y2b = wsb("y2b", [128, 4, WT], BF16)
            kdecT = wsb("kdecT", [128, 4, WT], BF16)
            qdecT = wsb("qdecT", [128, 4, WT], BF16)
            ktm = wsb("ktm", [128, WT // 128, 512], BF16)
            vtm = wsb("vtm", [128, WT // 128, 512], BF16)
            attm = wsb("attm", [128, 2, 256], BF16)
            Utf = wsb("Utf", [128, 32, C8])
            Utfb = wsb("Utfb", [128, 32, C8], BF16)
            Xin = wsb("Xin", [128, 2, 16, C8], BF16)
            xfull = wsb("xfull", [128, 2, 16, C8 + 1])
            Fb = wsb("Fb", [128, 4, 4])

            tiles = [(i * WT, WT, "pre") for i in range(NPRE)] + [(HALF, WARM, "warm")] + [(HALF + WARM + i * WT, WT, "main") for i in range(NMAIN)]
            scur = [0]
            hh = lambda h, W: (h // 2, slice((h % 2) * 256, (h % 2) * 256 + W))

            def xload(ti):
                t0, W, kind = tiles[ti]
                S.dma("pool", xt[:, :, 0:W], xT.rearrange("(k p) t -> p k t", p=128)[:, :, t0:t0 + W], [], ["xt"])

            def pre(ti, staged=False, do_load=True):
                t0, W, kind = tiles[ti]
                xn, xk = XN[ti % 2], f"xn{ti % 2}"
                if do_load:
                    xload(ti)
                if staged:
                    S.begin("Jb")
                S.op("act", lambda e: e.activation(out=xsq[:, :, 0:W], in_=xt[:, :, 0:W], func=AF.Square), ["xt"], ["xsq"])
                for k in range(8):
                    S.op("pe", lambda e, k=k: e.matmul(ps[7][:, 0:W], ones_b[:], xsq[:, k, 0:W], start=(k == 0), stop=(k == 7)), ["xsq", "ones_b"], [P[7]])
                S.op("act", lambda e: e.activation(out=rstd[:, 1, 0:W], in_=ps[7][:, 0:W], func=AF.Ln, scale=1.0 / D, bias=epsc), [P[7], "cst"], ["rstd1"])
                S.op("act", lambda e: e.activation(out=rstd[:, 0, 0:W], in_=rstd[:, 1, 0:W], func=AF.Exp, scale=-0.5), ["rstd1"], ["rstd0"])
                for k in range(8):
                    S.op("dve", lambda e, k=k: e.scalar_tensor_tensor(out=xn[:, k, 0:W], in0=xt[:, k, 0:W], scalar=g_in[:, k:k + 1], in1=rstd[:, 0, 0:W],
                                                                       op0=ALU.mult, op1=ALU.mult), ["xt", "rstd0", "sm"], [xk])

            def proj(xn, xk, col0, W, pt, pk, perm=False):
                for k in range(8):
                    rhs = xn[:, k, 0:W]
                    if perm:
                        rhs = rhs.rearrange("p (c s) -> p s c", s=8)
                    S.op("pe", lambda e, k=k, rhs=rhs: e.matmul(pt, w_in_sb[:, k, col0:col0 + 128], rhs, start=(k == 0), stop=(k == 7)), [xk, "w_in_sb"], [pk])

            def hstream(ti):
                t0, W, kind = tiles[ti]
                full = kind != "pre"
                nch = W // 64
                nsub = max(1, W // 128)
                subw = min(W, 128)
                xn, xk = XN[ti % 2], f"xn{ti % 2}"
                mix, mka = MIX[ti % 2], f"mixa{ti % 2}"
                E1, L1, L2, B1, SG = Fs[0], Fs[1], Fs[2], Fs[3], Fs[4]
                zz = lambda h: ps[h // 2][:, (h % 2) * 256:(h % 2) * 256 + W]
                for h in range(4):
                    proj(xn, xk, 512 + h * 128, W, zz(h), P[h // 2])
                for h in range(4):
                    S.op("act", lambda e, h=h: e.activation(out=E1[:, h, 0:W], in_=zz(h), func=AF.Exp, scale=-1.0), [P[h // 2]], ["F0"])
                for h in range(4):
                    S.op("act", lambda e, h=h: e.activation(out=L1[:, h, 0:W], in_=E1[:, h, 0:W], func=AF.Ln, scale=lb[:, h:h + 1], bias=1.0), ["F0", "cst"], ["F1"])
                S.op("act", lambda e: e.activation(out=L2[:, :, 0:W], in_=E1[:, :, 0:W], func=AF.Ln, scale=1.0, bias=1.0), ["F0"], ["F2"])
                S.op("dve", lambda e: e.tensor_tensor(out=L1[:, :, 0:W], in0=L1[:, :, 0:W], in1=L2[:, :, 0:W], op=ALU.subtract), ["F1", "F2"], ["F1"])
                for h in range(4):
                    S.op("dve", lambda e, h=h: e.tensor_tensor_scan(out=B1[:, h, 0:W], data0=mreset[:, 0:W], data1=L1[:, h, 0:W], initial=0.0,
                                                                     op0=ALU.mult, op1=ALU.add), ["F1", "mreset"], ["F3"])
                for h in range(4):
                    S.op("dve", lambda e, h=h: e.tensor_tensor(out=L2[:, h, 0:W], in0=zz(h), in1=L2[:, h, 0:W], op=ALU.add), [P[h // 2], "F2"], ["F2"])
                S.op("dve", lambda e: e.tensor_tensor(out=L2[:, :, 0:W], in0=L2[:, :, 0:W], in1=B1[:, :, 0:W], op=ALU.add), ["F2", "F3"], ["F2"])
                S.op("act", lambda e: e.activation(out=E1[:, :, 0:W], in_=B1[:, :, 0:W], func=AF.Exp), ["F3"], ["F0"])
                for h in range(4):
                    S.op("act", lambda e, h=h: e.activation(out=kdecT[:, h, 0:W], in_=L2[:, h, 0:W], func=AF.Exp, scale=-1.0, bias=lnoml[:, h:h + 1]),
                         ["F2", "cst"], ["kdecT"])
                if full:
                    for h in range(4):
                        S.op("dve", lambda e, h=h: e.tensor_tensor(out=ktT[:, h, 0:W].rearrange("p (n c) -> p n c", c=64),
                                                                   in0=kdecT[:, h, 0:W].rearrange("p (n c) -> p n c", c=64),
                                                                   in1=bc(E1[:, h, 0:W].rearrange("p (n c) -> p n c", c=64)[:, :, 63:64], [128, nch, 64]),
                                                                   op=ALU.mult), ["kdecT", "F0"], ["ktT"])
                else:
                    Ev = lambda n: E1[:, :, n * 64 + 63]
                    S.op("dve", lambda e: e.tensor_copy(out=Fb[:, :, nch - 1], in_=Ev(nch - 1)), ["F0"], ["Fb"])
                    for n in range(nch - 2, -1, -1):
                        S.op("dve", lambda e, n=n: e.tensor_tensor(out=Fb[:, :, n], in0=Ev(n), in1=Fb[:, :, n + 1], op=ALU.mult), ["F0", "Fb"], ["Fb"])
                    for h in range(4):
                        S.op("dve", lambda e, h=h: e.tensor_tensor(out=ktT[:, h, 0:W].rearrange("p (n c) -> p n c", c=64),
                                                                   in0=kdecT[:, h, 0:W].rearrange("p (n c) -> p n c", c=64),
                                                                   in1=bc(Fb[:, h, 0:nch].unsqueeze(2), [128, nch, 64]),
                                                                   op=ALU.mult), ["kdecT", "Fb"], ["ktT"])
                for s_ in range(nsub):
                    pt = ps[2 + s_]
                    for k in range(8):
                        S.op("pe", lambda e, k=k, s_=s_, pt=pt: e.matmul(pt[0:subw, 0:512], xn[:, k, s_ * 128:s_ * 128 + subw], w_in_sb[:, k, 1024:1536],
                                                                          start=(k == 0), stop=(k == 7)), [xk, "w_in_sb"], [P[2 + s_]])
                    S.op("act", lambda e, s_=s_, pt=pt: e.activation(out=vtm[0:subw, s_, :], in_=pt[0:subw, 0:512], func=AF.Copy), [P[2 + s_]], ["vtm"])
                for s_ in range(nsub):
                    ptb = ps[2 + s_][:].bitcast(BF16)
                    for h in range(4):
                        S.op("pe", lambda e, s_=s_, h=h, ptb=ptb: e.transpose(ptb[0:subw, h * 128:(h + 1) * 128], ktT[:, h, s_ * 128:s_ * 128 + subw], ident_b[:]),
                             ["ktT", "ident_b"], [P[2 + s_]])
                    S.op("act", lambda e, s_=s_, ptb=ptb: e.activation(out=ktm[0:subw, s_, :], in_=ptb[0:subw, 0:512], func=AF.Copy), [P[2 + s_]], ["ktm"])
                if not full:
                    for par in range(min(2, nch)):
                        ns = list(range(par, nch, 2))
                        for h in range(4):
                            for n in ns:
                                S.op("pe", lambda e, n=n, h=h, par=par, ns=ns: e.matmul(ps[par][:, h * 128:(h + 1) * 128], ktm[par * 64:par * 64 + 64, n // 2, h * 128:(h + 1) * 128],
                                                                                         vtm[par * 64:par * 64 + 64, n // 2, h * 128:(h + 1) * 128],
                                                                                         start=(n == ns[0]), stop=(n == ns[-1])), ["ktm", "vtm"], [P[par]])
                    So, Sn = Sst[scur[0]], Sst[1 - scur[0]]
                    ko, kn = f"Sst{scur[0]}", f"Sst{1 - scur[0]}"
                    S.op("dve", lambda e, So=So, Sn=Sn: e.tensor_tensor(out=Sn[:].rearrange("p (h v) -> p h v", h=4), in0=So[:].rearrange("p (h v) -> p h v", h=4),
                                                                        in1=bc(Fb[:, :, 0:1], [128, 4, 128]), op=ALU.mult), [ko, "Fb"], [kn])
                    for par in range(min(2, nch)):
                        S.op("dve", lambda e, par=par, Sn=Sn: e.tensor_tensor(out=Sn[:], in0=ps[par][:, :], in1=Sn[:], op=ALU.add), [kn, P[par]], [kn])
                    scur[0] = 1 - scur[0]
                    S.op("act", lambda e, Sn=Sn: e.activation(out=S_bf[:, 0, :], in_=Sn[:], func=AF.Copy), [kn], ["S_bf"])
                    return
                for n in range(nch):
                    s_, half = n // 2, (n % 2) * 64
                    pk = 2 + (n % 2)
                    for h in range(4):
                        S.op("pe", lambda e, n=n, h=h, s_=s_, half=half, pk=pk: e.matmul(ps[pk][:, h * 128:(h + 1) * 128], ktm[half:half + 64, s_, h * 128:(h + 1) * 128],
                                                                                        vtm[half:half + 64, s_, h * 128:(h + 1) * 128], start=True, stop=True),
                             ["ktm", "vtm"], [P[pk]])
                    So, Sn = Sst[scur[0]], Sst[1 - scur[0]]
                    ko, kn = f"Sst{scur[0]}", f"Sst{1 - scur[0]}"
                    S.op("dve", lambda e, n=n, So=So, Sn=Sn: e.tensor_tensor(out=Sn[:].rearrange("p (h v) -> p h v", h=4), in0=So[:].rearrange("p (h v) -> p h v", h=4),
                                                                             in1=bc(E1[:, :, n * 64 + 63:n * 64 + 64], [128, 4, 128]), op=ALU.mult), [ko, "F0"], [kn])
                    S.op("dve", lambda e, Sn=Sn, pk=pk: e.tensor_tensor(out=Sn[:], in0=ps[pk][:, :], in1=Sn[:], op=ALU.add), [kn, P[pk]], [kn])
                    scur[0] = 1 - scur[0]
                    if full or n == nch - 1:
                        S.op("act", lambda e, Sn=Sn, dst=n + 1: e.activation(out=S_bf[:, dst, :], in_=Sn[:], func=AF.Copy), [kn], ["S_bf"])
                if not full:
                    S.op("act", lambda e: e.activation(out=S_bf[:, 0, :], in_=S_bf[:, nch, :], func=AF.Copy), ["S_bf"], ["S_bf"])
                    return
                for h in range(4):
                    proj(xn, xk, h * 128, W, zz(h), P[h // 2])
                    S.op("dve", lambda e, h=h: e.tensor_tensor(out=qdecT[:, h, 0:W], in0=zz(h), in1=E1[:, h, 0:W], op=ALU.mult), [P[h // 2], "F0"], ["qdecT"])
                gg = lambda h: ps[2 + h // 2][:, (h % 2) * 256:(h % 2) * 256 + W]
                for h in range(4):
                    proj(xn, xk, 1536 + h * 128, W, gg(h), P[2 + h // 2])
                    S.op("act", lambda e, h=h: e.activation(out=SG[:, h, 0:W], in_=gg(h), func=AF.Silu), [P[2 + h // 2]], ["F4"])
                oo = lambda h, n: ps[2 + h // 2][:, (h % 2) * 256 + n * 64:(h % 2) * 256 + (n + 1) * 64]
                for pr in range((nch + 1) // 2):
                    for n in range(2 * pr, min(2 * pr + 2, nch)):
                        half = (n % 2) * 64
                        for h in range(4):
                            S.op("pe", lambda e, n=n, h=h, half=half: e.matmul(ps[0][half:half + 64, h * 64:(h + 1) * 64], kdecT[:, h, n * 64:(n + 1) * 64],
                                                                              qdecT[:, h, n * 64:(n + 1) * 64], start=True, stop=True), ["kdecT", "qdecT"], [P[0]])
                    np_ = 128 if 2 * pr + 1 < nch else 64
                    S.op("dve", lambda e, pr=pr, np_=np_: e.tensor_tensor(out=attm[0:np_, pr % 2, :].rearrange("p (h c) -> p h c", h=4),
                                                                          in0=ps[0][0:np_, 0:256].rearrange("p (h c) -> p h c", h=4),
                                                                          in1=bc(mask2[0:np_, :].unsqueeze(1), [np_, 4, 64]), op=ALU.mult), [P[0], "mask2"], [f"attm{pr % 2}"])
                    for n in range(2 * pr, min(2 * pr + 2, nch)):
                        half = (n % 2) * 64
                        s_ = n // 2
                        for h in range(4):
                            po = oo(h, n)
                            S.op("pe", lambda e, n=n, h=h, po=po: e.matmul(po, S_bf[:, n, h * 128:(h + 1) * 128], qdecT[:, h, n * 64:(n + 1) * 64], start=True, stop=False),
                                 ["S_bf", "qdecT"], [P[2 + h // 2]])
                            S.op("pe", lambda e, n=n, h=h, po=po, half=half, s_=s_, pr=pr: e.matmul(po, vtm[half:half + 64, s_, h * 128:(h + 1) * 128],
                                                                                                 attm[half:half + 64, pr % 2, h * 64:(h + 1) * 64], start=False, stop=True),
                                 ["vtm", f"attm{pr % 2}"], [P[2 + h // 2]])
                S.op("act", lambda e: e.activation(out=S_bf[:, 0, :], in_=S_bf[:, nch, :], func=AF.Copy), ["S_bf"], ["S_bf"])
                ov = lambda h: ps[2 + h // 2][:, (h % 2) * 256:(h % 2) * 256 + W]
                for h in range(4):
                    S.op("act", lambda e, h=h: e.activation(out=ktT[:, h, 0:W], in_=ov(h), func=AF.Square), [P[2 + h // 2]], ["ktT"])
                OT = Fs[1]
                for h in range(4):
                    pk = h % 2
                    S.op("pe", lambda e, h=h, pk=pk: e.matmul(ps[pk][:, 0:W], ones_b[:], ktT[:, h, 0:W], start=True, stop=True), ["ktT", "ones_b"], [P[pk]])
                    S.op("act", lambda e, h=h, pk=pk: e.activation(out=OT[:, h, 0:W], in_=ps[pk][:, 0:W], func=AF.Ln, scale=1.0 / 128, bias=epsc), [P[pk], "cst"], ["F1"])
                S.op("act", lambda e: e.activation(out=OT[:, :, 0:W], in_=OT[:, :, 0:W], func=AF.Exp, scale=-0.5), ["F1"], ["F1"])
                for h in range(4):
                    S.op("dve", lambda e, h=h: e.scalar_tensor_tensor(out=OT[:, h, 0:W], in0=ov(h), scalar=ngv[:, h:h + 1], in1=OT[:, h, 0:W],
                                                                       op0=ALU.mult, op1=ALU.mult), [P[2 + h // 2], "F1", "sm"], ["F1"])
                S.op("dve", lambda e: e.tensor_tensor(out=mix[:, 0:4, 0:W], in0=OT[:, :, 0:W], in1=SG[:, :, 0:W], op=ALU.mult), ["F1", "F4"], [mka])

            def sstream(ti, n2, n3):
                t0, W, kind = tiles[ti]
                full = kind != "pre"
                C = W // 8
                xn, xk = XN[ti % 2], f"xn{ti % 2}"
                mix, mkb = MIX[ti % 2], f"mixb{ti % 2}"
                Usb, uk = (Fs[5], "F5") if ti % 2 == 0 else (Fs[9], "F9")
                uu = lambda kk: ps[6 + kk // 2][:, (kk % 2) * 256:(kk % 2) * 256 + W]
                for kk in range(4):
                    proj(xn, xk, 2048 + kk * 128, W, uu(kk), P[6 + kk // 2], perm=True)
                for kk in range(4):
                    S.op("act", lambda e, kk=kk: e.activation(out=Usb[:, kk, 0:W], in_=uu(kk), func=AF.Copy), [P[6 + kk // 2]], [uk])
                S.end(); S.begin("S1xb")
                S.dma("sp", uscr.rearrange("(k p) t -> p k t", p=128)[:, :, 0:W], Usb[:, :, 0:W], [uk], ["uscr"])
                uv = uscr[:, 0:W].rearrange("(g p) (s c) -> p g s c", p=16, s=8)
                for s_ in range(8):
                    S.dma("sp", Utf[s_ * 16:(s_ + 1) * 16, :, 0:C], uv[:, :, s_, :], ["uscr"], ["Utf"])
                S.end(); S.begin(n2 + "a")
                S.op("pool", lambda e: e.tensor_copy(out=Utfb[:, :, 0:C], in_=Utf[:, :, 0:C]), ["Utf"], ["Utfb"])
                S.end(); S.begin(n2 + "b")
                for g in range(32):
                    q_, gi = g // 2, g % 2
                    for (BAt, pk, nm) in ((BAre, 4, "BAre"), (BAim, 5, "BAim")):
                        S.op("pe", lambda e, g=g, q_=q_, gi=gi, pk=pk, BAt=BAt: e.matmul(ps[pk][gi * 64:gi * 64 + 64, q_ * C:(q_ + 1) * C], BAt[:, g, :], Utfb[:, g, 0:C],
                                                                                        start=True, stop=True), ["Utfb", nm], [P[pk]])
                BR, VR, XR = Fs[6], Fs[7], Fs[8]
                v4 = lambda t: t[:].rearrange("p a w -> p (a w)").rearrange("p (a q c) -> p a q c", a=2, q=16)
                brv, vrv, tmv = v4(BR), v4(VR), v4(XR)
                Lre = ps[4][:, 0:16 * C].rearrange("p (q c) -> p q c", q=16)
                Lim = ps[5][:, 0:16 * C].rearrange("p (q c) -> p q c", q=16)
                S.op("dve", lambda e: e.tensor_tensor(out=brv[:, 0, :, 0:C], in0=Lre, in1=rcos[:, :, 0:C], op=ALU.mult), [P[4], "rot"], ["F6"])
                S.op("dve", lambda e: e.tensor_tensor(out=tmv[:, 0, :, 0:C], in0=Lim, in1=rsin[:, :, 0:C], op=ALU.mult), [P[5], "rot"], ["F8"])
                S.op("dve", lambda e: e.tensor_tensor(out=brv[:, 1, :, 0:C], in0=Lim, in1=rcos[:, :, 0:C], op=ALU.mult), [P[5], "rot"], ["F6"])
                S.op("dve", lambda e: e.tensor_tensor(out=tmv[:, 1, :, 0:C], in0=Lre, in1=rsin[:, :, 0:C], op=ALU.mult), [P[4], "rot"], ["F8"])
                S.op("pool", lambda e: e.tensor_tensor(out=brv[:, 0, :, 0:C], in0=brv[:, 0, :, 0:C], in1=tmv[:, 0, :, 0:C], op=ALU.add), ["F6", "F8"], ["F6"])
                S.op("pool", lambda e: e.tensor_tensor(out=brv[:, 1, :, 0:C], in0=brv[:, 1, :, 0:C], in1=tmv[:, 1, :, 0:C], op=ALU.subtract), ["F6", "F8"], ["F6"])
                S.op("pool", lambda e: e.tensor_tensor(out=tmv[:, :, :, 0], in0=xc[:], in1=bc(r8[:].unsqueeze(1), [128, 2, 16]), op=ALU.mult), ["xc", "r8", "F8"], ["F8"])
                S.op("pool", lambda e: e.tensor_tensor(out=brv[:, :, :, 0], in0=brv[:, :, :, 0], in1=tmv[:, :, :, 0], op=ALU.add), ["F6", "F8"], ["F6"])
                for a_ in range(2):
                    if C == C8:
                        d0 = R8T[:].rearrange("p q c -> p (q c)"); d1 = brv[:, a_].rearrange("p q c -> p (q c)"); oo_ = vrv[:, a_].rearrange("p q c -> p (q c)")
                        S.op("dve", lambda e, d0=d0, d1=d1, oo_=oo_: e.tensor_tensor_scan(out=oo_, data0=d0, data1=d1, initial=0.0, op0=ALU.mult, op1=ALU.add),
                             ["F6", "R8T"], ["F7"])
                    else:
                        for q_ in range(16):
                            S.op("dve", lambda e, a_=a_, q_=q_: e.tensor_tensor_scan(out=vrv[:, a_, q_, 0:C], data0=R8T[:, q_, 0:C], data1=brv[:, a_, q_, 0:C], initial=0.0,
                                                                                     op0=ALU.mult, op1=ALU.add), ["F6", "R8T"], ["F7"])
                S.op("pool", lambda e: e.tensor_copy(out=xfull[:, :, :, 0], in_=xc[:]), ["xc"], ["xfull"])
                S.op("dve", lambda e: e.tensor_tensor(out=tmv[:, 0, :, 0:C], in0=vrv[:, 1, :, 0:C], in1=rsin[:, :, 0:C], op=ALU.mult), ["F7", "rot", "F8"], ["F8"])
                S.op("pool", lambda e: e.tensor_tensor(out=tmv[:, 1, :, 0:C], in0=vrv[:, 0, :, 0:C], in1=rsin[:, :, 0:C], op=ALU.mult), ["F7", "rot", "F8"], ["F8"])
                S.op("dve", lambda e: e.tensor_tensor(out=brv[:, 0, :, 0:C], in0=vrv[:, 0, :, 0:C], in1=rcos[:, :, 0:C], op=ALU.mult), ["F7", "rot", "F6"], ["F6"])
                S.op("pool", lambda e: e.tensor_tensor(out=brv[:, 1, :, 0:C], in0=vrv[:, 1, :, 0:C], in1=rcos[:, :, 0:C], op=ALU.mult), ["F7", "rot", "F6"], ["F6"])
                S.op("dve", lambda e: e.tensor_tensor(out=xfull[:, 0, :, 1:C + 1], in0=brv[:, 0, :, 0:C], in1=tmv[:, 0, :, 0:C], op=ALU.subtract), ["F6", "F8", "xfull"], ["xfull"])
                S.op("pool", lambda e: e.tensor_tensor(out=xfull[:, 1, :, 1:C + 1], in0=brv[:, 1, :, 0:C], in1=tmv[:, 1, :, 0:C], op=ALU.add), ["F6", "F8", "xfull"], ["xfull"])
                S.op("pool", lambda e: e.tensor_copy(out=xc[:], in_=xfull[:, :, :, C]), ["xfull"], ["xc"])
                if not full:
                    return
                S.end(); S.begin(n2 + "c")
                S.op("pool", lambda e: e.tensor_copy(out=Xin[:, :, :, 0:C], in_=xfull[:, :, :, 0:C]), ["xfull"], ["Xin"])
                for g in range(32):
                    q_, gi = g // 2, g % 2
                    rs = slice(gi * 64, gi * 64 + 64)
                    pk = 4 + g // 16
                    po = ps[pk][:, (g % 16) * C:(g % 16 + 1) * C]
                    S.op("pe", lambda e, g=g, po=po: e.matmul(po, Kt[:, g, :], Utfb[:, g, 0:C], start=True, stop=False), ["Kt", "Utfb"], [P[pk]])
                    S.op("pe", lambda e, q_=q_, rs=rs, po=po: e.matmul(po, CAre[rs, q_, :], Xin[rs, 0, q_, 0:C], start=False, stop=False), ["CAre", "Xin"], [P[pk]])
                    S.op("pe", lambda e, q_=q_, rs=rs, po=po: e.matmul(po, CAimn[rs, q_, :], Xin[rs, 1, q_, 0:C], start=False, stop=True), ["CAimn", "Xin"], [P[pk]])
                Ytf = Fs[6]
                ytv = Ytf[:].rearrange("p a w -> p (a w)")[:, 0:32 * C].rearrange("p (g c) -> p g c", g=32)
                for hb in range(2):
                    S.op("act", lambda e, hb=hb: e.activation(out=ytv[:, hb * 16:(hb + 1) * 16, :], in_=ps[4 + hb][:, 0:16 * C].rearrange("p (g c) -> p g c", g=16), func=AF.Copy),
                         [P[4 + hb], "F6"], ["F6"])
                yv = yscr[:, 0:W].rearrange("(g p) (s c) -> p g s c", p=16, s=8)
                for s_ in range(8):
                    S.dma("sp", yv[:, :, s_, :], ytv[s_ * 16:(s_ + 1) * 16, :, :], ["F6"], ["yscr"])
                Yp = Fs[7]
                S.dma("sp", Yp[:, :, 0:W], yscr.rearrange("(k p) t -> p k t", p=128)[:, :, 0:W], ["yscr"], ["F7"])
                S.end(); S.begin(n3 + "a")
                for kk in range(4):
                    S.op("dve", lambda e, kk=kk: e.scalar_tensor_tensor(out=Yp[:, kk, 0:W], in0=Usb[:, kk, 0:W], scalar=dsk[:, kk:kk + 1], in1=Yp[:, kk, 0:W],
                                                                         op0=ALU.mult, op1=ALU.add), [uk, "F7", "sm"], ["F7"])
                Y2 = Fs[8]
                S.op("act", lambda e: e.activation(out=Y2[:, :, 0:W], in_=Yp[:, :, 0:W], func=AF.Gelu_apprx_tanh), ["F7"], ["F8"])
                S.op("pool", lambda e: e.tensor_copy(out=y2b[:, :, 0:W], in_=Y2[:, :, 0:W]), ["F8"], ["y2b"])
                S.op("pool", lambda e: e.tensor_scalar(out=Yp[:, :, 0:W], in0=Y2[:, :, 0:W], scalar1=0.5, scalar2=None, op0=ALU.mult), ["F8", "F7"], ["F7"])
                TH = Fs[6]
                S.end(); S.begin(n3 + "b")
                for jo in range(4):
                    pg = ps[6][:, (jo % 2) * 256:(jo % 2) * 256 + W]
                    for kk in range(4):
                        S.op("pe", lambda e, jo=jo, kk=kk, pg=pg: e.matmul(pg, w_glu_sb[:, kk, jo * 128:(jo + 1) * 128], y2b[:, kk, 0:W], start=(kk == 0), stop=(kk == 3)),
                             ["y2b", "w_glu_sb"], [P[6]])
                    S.op("act", lambda e, jo=jo, pg=pg: e.activation(out=TH[:, jo, 0:W], in_=pg, func=AF.Tanh, scale=0.5, bias=hbg[:, jo:jo + 1]), [P[6], "cst"], ["F6"])
                S.op("dve", lambda e: e.scalar_tensor_tensor(out=mix[:, 4:8, 0:W], in0=TH[:, :, 0:W], scalar=1.0, in1=Yp[:, :, 0:W], op0=ALU.add, op1=ALU.mult),
                     ["F6", "F7"], [mkb])

            def xreload(ti):
                t0, W, kind = tiles[ti]
                S.dma("pool", HMs[:, :, 0:W], xT.rearrange("(k p) t -> p k t", p=128)[:, :, t0:t0 + W], [], ["HMs"])

            def jstream(ti, staged=True):
                t0, W, kind = tiles[ti]
                mix, mka, mkb = MIX[ti % 2], f"mixa{ti % 2}", f"mixb{ti % 2}"
                hoff = t0 - HALF
                S.begin("Jc1")
                for dc in range(8):
                    if dc == 4:
                        S.end(); S.begin("Jc2")
                    pj = ps[7][:, (dc % 2) * 256:(dc % 2) * 256 + W]
                    for k in range(8):
                        rhs = mix[:, k, 0:W]
                        if k >= 4:
                            rhs = rhs.rearrange("p (s c) -> p c s", s=8)
                        S.op("pe", lambda e, dc=dc, k=k, pj=pj, rhs=rhs: e.matmul(pj, w_out_sb[:, k, dc * 128:(dc + 1) * 128], rhs, start=(k == 0), stop=(k == 7)),
                             [mka, mkb, "w_out_sb"], [P[7]])
                    S.op("dve", lambda e, dc=dc, pj=pj: e.tensor_tensor(out=HMs[:, dc, 0:W], in0=pj, in1=HMs[:, dc, 0:W], op=ALU.add), [P[7], "HMs"], ["HMs"])
                S.dma("pool", hmid.rearrange("(k p) t -> p k t", p=128)[:, :, hoff:hoff + W], HMs[:, :, 0:W], ["HMs"], ["hmid"])

            nt1 = len(tiles)
            pre(0)
            xload(1)
            S.begin("S1xa"); sstream(0, "S2_0", "S3_0"); S.end()
            S.merge(["S1xa", "S1xb"], [(0.0, 0.5), (0.5, 1.0)])
            for ti in range(nt1 + 1):
                if ti < nt1:
                    S.begin("H"); hstream(ti); S.end()
                    if ti + 1 < nt1:
                        S.begin("S1xa"); sstream(ti + 1, f"S2_{ti + 1}", f"S3_{ti + 1}"); S.end()
                        pre(ti + 1, staged=True, do_load=False)
                        S.end()
                    if ti + 2 < nt1:
                        S.begin("Ja"); xload(ti + 2); S.end()
                    if tiles[ti][2] != "pre":
                        S.begin("Jc0"); xreload(ti); S.end()
                hasj = ti >= 1 and tiles[ti - 1][2] != "pre"
                if hasj:
                    jstream(ti - 1); S.end()
                n2, n3 = f"S2_{ti}", f"S3_{ti}"
                names = ["H", n2 + "a", n2 + "b", n2 + "c", n3 + "a", n3 + "b", "S1xa", "S1xb", "Ja", "Jb", "Jc0", "Jc1", "Jc2"]
                if ti < nt1 and tiles[ti][2] == "pre":
                    spans = [(0.0, 1.0), (0.0, 0.02), (0.4, 0.85), (0.85, 0.85), (0.85, 0.85), (0.85, 0.85), (0.3, 0.38), (0.85, 1.0),
                             (0.6, 0.61), (0.0, 0.2), (0.7, 0.71), (0.25, 0.3), (0.6, 0.66)]
                else:
                    spans = [(0.0, 1.0), (0.0, 0.02), (0.2, 0.42), (0.47, 0.58), (0.68, 0.74), (0.78, 0.88), (0.43, 0.47), (0.88, 1.0),
                             (0.6, 0.61), (0.0, 0.1), (0.7, 0.71), (0.12, 0.18), (0.6, 0.66)]
                S.merge(names, spans)
            assert not any(S.streams.values()), [k for k, v in S.streams.items() if v]
            S.barrier()
        pw.close()
        p2 = ExitStack()
        with p2:
            def vsb(name, shape, dt=F32):
                return p2.enter_context(nc.sbuf_tensor(name, list(shape), dt))
            WF = 512
            w_up_sb = vsb("w_up_sb", [128, 8, 2 * DFF], BF16)
            wsrc = w_up.rearrange("(k p) n -> p k n", p=128)
            for cbk in (0, 5, 6, 1, 7, 2, 8, 3, 9, 4, 10):
                S.dma("pool", w_up_sb[:, :, cbk * 512:(cbk + 1) * 512], wsrc[:, :, cbk * 512:(cbk + 1) * 512], [], [f"w_up{cbk}"])
            w_down_sb = vsb("w_down_sb", [128, 22, D], BF16)
            S.dma("pool", w_down_sb[:], w_down.rearrange("(j p) n -> p j n", p=128), [], ["w_down_sb"])
            hm = [vsb(f"hm{i}", [128, 8, WF]) for i in range(2)]
            hn = vsb("hn", [128, 8, WF], BF16)
            hsq = hn
            ag = [vsb(f"ag{i}", [128, WF]) for i in range(2)]
            av = [vsb(f"av{i}", [128, WF]) for i in range(2)]
            actb = vsb("actb", [128, 22, WF], BF16)
            Hh = vsb("Hh", [128, NFC, 2])
            corr = vsb("corr", [128, NFC, 2])
            ctmp = vsb("ctmp", [128, NFC])
            S.op("pool", lambda e: e.memset(Hh[:], 0.0), [], ["Hh"])
            AK = lambda j: "actb_lo" if j < 14 else "actb_hi"

            def ffn_norm(src, srck, sq, sqk, pk, W):
                S.op("act", lambda e: e.activation(out=sq[:, :, 0:W], in_=src[:, :, 0:W], func=AF.Square), [srck], [sqk])
                for k in range(8):
                    S.op("pe", lambda e, k=k: e.matmul(ps[pk][:, 0:W], ones_b[:], sq[:, k, 0:W], start=(k == 0), stop=(k == 7)), [sqk, "ones_b"], [P[pk]])
                S.op("act", lambda e: e.activation(out=ps[pk][:, 0:W], in_=ps[pk][:, 0:W], func=AF.Ln, scale=1.0 / D, bias=epsc), [P[pk], "cst"], [P[pk]])
                S.op("act", lambda e: e.activation(out=ps[pk][:, 0:W], in_=ps[pk][:, 0:W], func=AF.Exp, scale=-0.5), [P[pk]], [P[pk]])

            ftiles = [(0, WARM, "warm")] + [(WARM + i * WF, WF, "main") for i in range(HALF // WF)]

            def prep(ti, part="AB", early=False):
                t0, W, kind = ftiles[ti]
                hb, hk = hm[ti % 2], f"hm{ti % 2}"
                if "A" in part:
                    S.dma("sp", hb[:, :, 0:W], hmid.rearrange("(k p) t -> p k t", p=128)[:, :, t0:t0 + W], [], [hk])
                    if early:
                        ffn_norm(hb, hk, actb[:, 14:22, :], "actb_hi", 7, W)
                    else:
                        ffn_norm(hb, hk, hsq, "hn", 7, W)
                if "B" not in part:
                    return
                for k in range(8):
                    S.op("dve", lambda e, k=k: e.scalar_tensor_tensor(out=hn[:, k, 0:W], in0=hb[:, k, 0:W], scalar=g_ffn[:, k:k + 1], in1=ps[7][:, 0:W],
                                                                       op0=ALU.mult, op1=ALU.mult), [hk, P[7], "sm"], ["hn"])

            def up(ti):
                t0, W, kind = ftiles[ti]
                main = kind == "main"
                if main:
                    S.op("dve", lambda e: e.tensor_tensor(out=ctmp[:], in0=Hh[:, :, 1], in1=cw[:, 1, :], op=ALU.mult), ["Hh", "sm"], ["ctmp"])
                    S.op("dve", lambda e: e.tensor_tensor(out=corr[:, :, 0], in0=Hh[:, :, 0], in1=cw[:, 0, :], op=ALU.mult), ["Hh", "sm"], ["corr"])
                    S.op("dve", lambda e: e.tensor_tensor(out=corr[:, :, 0], in0=corr[:, :, 0], in1=ctmp[:], op=ALU.add), ["corr", "ctmp"], ["corr"])
                    S.op("dve", lambda e: e.tensor_tensor(out=corr[:, :, 1], in0=Hh[:, :, 1], in1=cw[:, 0, :], op=ALU.mult), ["Hh", "sm", "corr"], ["corr"])
                for j in range(22):
                    sl = j % 2
                    for (c, abuf, nm, pk) in ((j, ag[sl], f"ag{sl}", (j % 3) * 2), (22 + j, av[sl], f"av{sl}", (j % 3) * 2 + 1)):
                        pt = ps[pk]
                        for k in range(8):
                            S.op("pe", lambda e, k=k, c=c, pt=pt: e.matmul(pt[:, 0:W], w_up_sb[:, k, c * 128:(c + 1) * 128], hn[:, k, 0:W], start=(k == 0), stop=(k == 7)),
                                 ["hn", f"w_up{c // 4}"], [P[pk]])
                        if main:
                            S.op("act", lambda e, c=c, pt=pt, abuf=abuf: e.activation(out=abuf[:, 0:W], in_=pt[:, 0:W], func=AF.Identity, scale=cw[:, 2, c:c + 1], bias=cb[:, c:c + 1]),
                                 [P[pk], "sm"], [nm])
                            S.op("dve", lambda e, c=c, pt=pt, abuf=abuf: e.scalar_tensor_tensor(out=abuf[:, 1:W], in0=pt[:, 0:W - 1], scalar=cw[:, 1, c:c + 1], in1=abuf[:, 1:W],
                                                                                                 op0=ALU.mult, op1=ALU.add), [P[pk], "sm", nm], [nm])
                            S.op("dve", lambda e, c=c, pt=pt, abuf=abuf: e.scalar_tensor_tensor(out=abuf[:, 2:W], in0=pt[:, 0:W - 2], scalar=cw[:, 0, c:c + 1], in1=abuf[:, 2:W],
                                                                                                 op0=ALU.mult, op1=ALU.add), [P[pk], "sm", nm], [nm])
                            S.op("pool", lambda e, c=c, abuf=abuf: e.tensor_tensor(out=abuf[:, 0:2], in0=abuf[:, 0:2], in1=corr[:, c, :], op=ALU.add), ["corr", nm], [nm])
                        S.op("act", lambda e, c=c, pt=pt: e.activation(out=Hh[:, c, :], in_=pt[:, W - 2:W], func=AF.Copy), [P[pk], "Hh"], ["Hh"])
                    if main:
                        S.op("act", lambda e, sl=sl: e.activation(out=ag[sl][:, 0:W], in_=ag[sl][:, 0:W], func=AF.Silu), [f"ag{sl}"], [f"ag{sl}"])
                        S.op("pool", lambda e, sl=sl, j=j: e.tensor_tensor(out=actb[:, j, 0:W], in0=ag[sl][:, 0:W], in1=av[sl][:, 0:W], op=ALU.mult), [f"ag{sl}", f"av{sl}"], [AK(j)])

            def down(ti):
                t0, W, kind = ftiles[ti]
                hb, hk = hm[ti % 2], f"hm{ti % 2}"
                for dc in range(8):
                    pk = 4 + dc % 2
                    for j in range(22):
                        S.op("pe", lambda e, dc=dc, j=j, pk=pk: e.matmul(ps[pk][:, 0:W], w_down_sb[:, j, dc * 128:(dc + 1) * 128], actb[:, j, 0:W], start=(j == 0), stop=(j == 21)),
                             [AK(j), "w_down_sb"], [P[pk]])
                    S.op("dve", lambda e, dc=dc, pk=pk: e.tensor_tensor(out=hb[:, dc, 0:W], in0=ps[pk][:, 0:W], in1=hb[:, dc, 0:W], op=ALU.add), [P[pk], hk], [hk])

            def fin(ti):
                t0, W, kind = ftiles[ti]
                hb, hk = hm[ti % 2], f"hm{ti % 2}"
                fsq = actb[:, 14:22, :]
                ffn_norm(hb, hk, fsq, "actb_hi", 6, W)
                for k in range(8):
                    S.op("dve", lambda e, k=k: e.scalar_tensor_tensor(out=hb[:, k, 0:W], in0=hb[:, k, 0:W], scalar=g_fin[:, k:k + 1], in1=ps[6][:, 0:W],
                                                                       op0=ALU.mult, op1=ALU.mult), [hk, P[6], "sm"], [hk])
                S.dma("sp", outT.rearrange("(k p) t -> p k t", p=128)[:, :, t0 - WARM:t0 - WARM + W], hb[:, :, 0:W], [hk], ["outT"])

            prep(0)
            up(0)
            prep(1)
            nt = len(ftiles)
            for ti in range(1, nt):
                S.begin("up"); up(ti); S.end()
                if ti + 1 < nt:
                    S.begin("pa"); prep(ti + 1, "A", early=True); S.end()
                S.merge(["up", "fin", "pa"], [(0.0, 1.0), (0.0, 0.2), (0.25, 0.45)])
                if ti + 1 < nt:
                    prep(ti + 1, "B")
                down(ti)
                S.begin("fin"); fin(ti); S.end()
            S.merge(["fin"])
            S.barrier()
            S.finalize()
            block = es.enter_context(nc.Block())
            block.sync(lambda e: S.replay("sp", e))
            block.tensor(lambda e: S.replay("pe", e))
            block.scalar(lambda e: S.replay("act", e))
            block.vector(lambda e: S.replay("dve", e))
            block.gpsimd(lambda e: S.replay("pool", e))
    return nc


def _pk(v, k):
    return np.ascontiguousarray(np.asarray(v, np.float32).reshape(k, 128).T)


def kernel(x, in_norm_g, w_in, hg_lb, hg_norm_g, s5_a_re, s5_a_im, s5_log_dt, s5_b_re, s5_b_im,
           s5_c_re, s5_c_im, s5_d, s5_w_glu, s5_b_glu, w_out, ffn_norm_g, w_up, conv_w, conv_b,
           w_down, final_norm_g):
    f = lambda a: np.asarray(a, np.float32)
    x = f(x)
    smalls = np.concatenate([
        _pk(f(in_norm_g)[0], 8), _pk(f(ffn_norm_g)[0], 8), _pk(f(final_norm_g), 8),
        np.concatenate([_pk(f(hg_lb)[0], 4), _pk(f(hg_lb)[1], 4)], axis=1),
        _pk(f(hg_norm_g)[0], 4), _pk(f(s5_d)[0], 4), _pk(f(s5_b_glu)[0], 4),
        np.concatenate([_pk(f(conv_w)[0, j], NFC) for j in range(3)], axis=1),
        _pk(f(conv_b)[0], NFC)], axis=1)

    def pairpack(a):
        a = f(a)
        r = a.reshape((16, 2, 64) + a.shape[2:])
        r = np.moveaxis(r, 0, 2)
        return np.ascontiguousarray(r.reshape((128, 16) + a.shape[2:]))
    are = pairpack(f(s5_a_re)[0]); aim = pairpack(f(s5_a_im)[0])
    ldt = pairpack(np.repeat(f(s5_log_dt)[0][:, None], 64, axis=1))
    s5s = np.concatenate([are, aim, ldt], axis=1)
    s5B = np.concatenate([pairpack(f(s5_b_re)[0]).reshape(128, -1), pairpack(f(s5_b_im)[0]).reshape(128, -1)], axis=1)
    ctre = np.transpose(f(s5_c_re)[0], (0, 2, 1)); ctim = np.transpose(f(s5_c_im)[0], (0, 2, 1))
    s5C = np.concatenate([pairpack(ctre).reshape(128, -1), pairpack(ctim).reshape(128, -1)], axis=1)
    common = {"w_in": f(w_in)[0], "w_out": f(w_out)[0], "w_glu": f(s5_w_glu)[0], "w_up": f(w_up)[0], "w_down": f(w_down)[0],
              "smalls": np.ascontiguousarray(smalls), "s5s": np.ascontiguousarray(s5s), "s5B": np.ascontiguousarray(s5B), "s5C": np.ascontiguousarray(s5C)}
    in_maps = []
    for c in range(8):
        b, h = c // 2, c % 2
        win = np.zeros((WIN, D), np.float32)
        start = h * HALF - (HALF + WARM)
        lo = max(start, 0)
        win[lo - start:] = x[b, lo:h * HALF + HALF]
        m = dict(common)
        m["xT"] = np.ascontiguousarray(win.T)
        in_maps.append(m)
    nc = build_program()
    res = run_bass_kernel_spmd(nc, in_maps, core_ids=list(range(8)))
    out = np.empty((4, SEQ, D), np.float32)
    for c in range(8):
        b, h = c // 2, c % 2
        out[b, h * HALF:(h + 1) * HALF] = res.results[c]["outT"].T
    return out
```
